# Optimizing a Trainium2 kernel written in Bass

```python
import jax, jax.numpy as jnp
from jax import lax
import numpy as np

D_MODEL = 1024
BATCH = 8
SEQ = 2048
DEPTH = 2

MIX_WIDTH = D_MODEL
DN_HEADS = 4
DN_HEAD_K = 128
DN_HEAD_V = 128
DN_K_WIDTH = DN_HEADS * DN_HEAD_K
DN_V_WIDTH = DN_HEADS * DN_HEAD_V
QKV_DIM = 2 * DN_K_WIDTH + DN_V_WIDTH
CONV_WIDTH = 4
DN_CHUNK = 64
GM_GROUPS = 4
GM_GROUP_DIM = 128
GM_WIDTH = GM_GROUPS * GM_GROUP_DIM
GM_CHUNK = 128
D_FF = -(-8 * D_MODEL // (3 * 256)) * 256
EPS = 1e-6

Q_OFF = 0
K_OFF = Q_OFF + DN_K_WIDTH
V_OFF = K_OFF + DN_K_WIDTH
Z_OFF = V_OFF + DN_V_WIDTH
BETA_OFF = Z_OFF + DN_V_WIDTH
A_OFF = BETA_OFF + DN_HEADS
GM_OFF = A_OFF + DN_HEADS
IN_DIM = GM_OFF + 2 * GM_WIDTH

kernel_name = "hybrid_gdn_gmlp_parallel_heads"


def _rmsnorm(x, g):
    xf = x.astype(jnp.float32)
    y = xf * lax.rsqrt(jnp.mean(xf * xf, axis=-1, keepdims=True) + EPS)
    return (y * g.astype(jnp.float32)).astype(x.dtype)


def _l2norm(x):
    return x * lax.rsqrt(jnp.sum(x * x, axis=-1, keepdims=True) + EPS)


def _causal_depthwise_conv(x, w):
    c = x.shape[-1]
    return lax.conv_general_dilated(
        x, w[:, None, :].astype(x.dtype), window_strides=(1,),
        padding=((CONV_WIDTH - 1, 0),), dimension_numbers=("NWC", "WIO", "NWC"),
        feature_group_count=c)


def _gated_delta_rule(q, k, v, g, beta):
    b, t, h, dk = q.shape
    dv = v.shape[-1]
    n = t // DN_CHUNK

    def chunks(a):
        a = jnp.moveaxis(a, 2, 1)
        return a.reshape((b, h, n, DN_CHUNK) + a.shape[3:])

    q, k, v, g, beta = (chunks(a) for a in (q, k, v, g, beta))
    g = jnp.cumsum(g, axis=-1)
    idx = jnp.arange(DN_CHUNK)
    incl = idx[:, None] >= idx[None, :]
    strict = idx[:, None] > idx[None, :]
    diff = g[..., :, None] - g[..., None, :]
    decay = jnp.where(incl, jnp.exp(jnp.where(incl, diff, 0.0)), 0.0)
    k_beta = k * beta[..., None]
    v_beta = v * beta[..., None]
    a_mat = jnp.where(strict, jnp.einsum("bhncd,bhnsd->bhncs", k_beta, k) * decay, 0.0)
    eye = jnp.eye(DN_CHUNK, dtype=a_mat.dtype)
    tri = a_mat + eye
    u = lax.linalg.triangular_solve(tri, v_beta, left_side=True, lower=True)
    w = lax.linalg.triangular_solve(tri, k_beta * jnp.exp(g)[..., None], left_side=True, lower=True)
    qk = jnp.einsum("bhncd,bhnsd->bhncs", q, k) * decay

    def step(s, xs):
        q_c, k_c, u_c, w_c, g_c, qk_c = xs
        v_new = u_c - jnp.einsum("bhcd,bhde->bhce", w_c, s)
        o = (jnp.einsum("bhcd,bhde->bhce", q_c * jnp.exp(g_c)[..., None], s)
             + jnp.einsum("bhcs,bhse->bhce", qk_c, v_new))
        g_last = g_c[..., -1]
        s = (s * jnp.exp(g_last)[..., None, None]
             + jnp.einsum("bhcd,bhce->bhde", k_c * jnp.exp(g_last[..., None] - g_c)[..., None], v_new))
        return s, o

    xs = tuple(jnp.moveaxis(a, 2, 0) for a in (q, k, u, w, g, qk))
    s0 = jnp.zeros((b, h, dk, dv), jnp.float32)
    _, o = lax.scan(step, s0, xs)
    o = jnp.moveaxis(o, 0, 2).reshape(b, h, t, dv)
    return jnp.moveaxis(o, 1, 2)


def _hybrid_mixer(hn, w_in, conv_w, a_log, dt_bias, o_norm_g, ln_v_g, ln_v_b, w_s, b_s, w_out):
    b, t, _ = hn.shape
    proj = hn @ w_in

    qkv = jax.nn.silu(_causal_depthwise_conv(proj[..., Q_OFF:Z_OFF], conv_w)).astype(jnp.float32)
    q = _l2norm(qkv[..., Q_OFF:K_OFF].reshape(b, t, DN_HEADS, DN_HEAD_K)) * (DN_HEAD_K ** -0.5)
    k = _l2norm(qkv[..., K_OFF:V_OFF].reshape(b, t, DN_HEADS, DN_HEAD_K))
    v = qkv[..., V_OFF:Z_OFF].reshape(b, t, DN_HEADS, DN_HEAD_V)
    z = proj[..., Z_OFF:BETA_OFF].astype(jnp.float32).reshape(b, t, DN_HEADS, DN_HEAD_V)
    beta = jax.nn.sigmoid(proj[..., BETA_OFF:A_OFF].astype(jnp.float32))
    g = -jnp.exp(a_log.astype(jnp.float32)) * jax.nn.softplus(
        proj[..., A_OFF:GM_OFF].astype(jnp.float32) + dt_bias.astype(jnp.float32))
    o = _gated_delta_rule(q, k, v, g, beta)
    o = o * lax.rsqrt(jnp.mean(o * o, axis=-1, keepdims=True) + EPS)
    o = o * o_norm_g.astype(jnp.float32) * jax.nn.silu(z)
    o_dn = o.reshape(b, t, DN_V_WIDTH).astype(hn.dtype)

    gm = jax.nn.gelu(proj[..., GM_OFF:IN_DIM])
    u_g = gm[..., :GM_WIDTH]
    v_g = gm[..., GM_WIDTH:].astype(jnp.float32).reshape(b, t, GM_GROUPS, GM_GROUP_DIM)
    mu = jnp.mean(v_g, axis=-1, keepdims=True)
    var = jnp.mean(jnp.square(v_g - mu), axis=-1, keepdims=True)
    v_g = ((v_g - mu) * lax.rsqrt(var + EPS) * ln_v_g.reshape(GM_GROUPS, GM_GROUP_DIM)
           + ln_v_b.reshape(GM_GROUPS, GM_GROUP_DIM)).astype(hn.dtype)
    v_g = v_g.reshape(b, t // GM_CHUNK, GM_CHUNK, GM_GROUPS, GM_GROUP_DIM)
    pos = jnp.arange(GM_CHUNK)
    ws = jnp.where(pos[:, None] >= pos[None, :], w_s, 0.0).astype(hn.dtype)
    sp = jnp.einsum("gts,bnsgc->bntgc", ws, v_g) + b_s.T[:, :, None].astype(hn.dtype)
    o_gm = u_g * sp.reshape(b, t, GM_WIDTH)

    return jnp.concatenate([o_dn, o_gm], axis=-1) @ w_out


def _swiglu(h, w_gate, w_up, w_down):
    return (jax.nn.silu(h @ w_gate) * (h @ w_up)) @ w_down


def setup_inputs(seed: int = 0) -> dict:
    key = jax.random.key(seed)
    ks = jax.random.split(key, 20)
    f32 = jnp.float32
    nrm = lambda k, shape, scale: jax.random.normal(k, shape, f32) * scale
    x = jax.random.normal(ks[0], (BATCH, SEQ, D_MODEL), f32)
    norm_mix = 1.0 + nrm(ks[1], (DEPTH, D_MODEL), 0.02)
    w_in = nrm(ks[2], (DEPTH, D_MODEL, IN_DIM), D_MODEL ** -0.5)
    conv_w = nrm(ks[3], (DEPTH, CONV_WIDTH, QKV_DIM), CONV_WIDTH ** -0.5)
    a_log = jnp.log(jax.random.uniform(ks[4], (DEPTH, DN_HEADS), f32, 1.0, 16.0))
    dt = jnp.exp(jax.random.uniform(ks[5], (DEPTH, DN_HEADS), f32, np.log(1e-3), np.log(1e-1)))
    dt_bias = dt + jnp.log(-jnp.expm1(-dt))
    o_norm_g = 1.0 + nrm(ks[6], (DEPTH, DN_HEAD_V), 0.02)
    ln_v_g = 1.0 + nrm(ks[7], (DEPTH, GM_WIDTH), 0.02)
    ln_v_b = nrm(ks[8], (DEPTH, GM_WIDTH), 0.02)
    w_s = nrm(ks[9], (DEPTH, GM_GROUPS, GM_CHUNK, GM_CHUNK), GM_CHUNK ** -0.5)
    b_s = 1.0 + nrm(ks[10], (DEPTH, GM_GROUPS, GM_CHUNK), 0.02)
    w_out = nrm(ks[11], (DEPTH, MIX_WIDTH, D_MODEL), MIX_WIDTH ** -0.5)
    norm_ffn = 1.0 + nrm(ks[12], (DEPTH, D_MODEL), 0.02)
    w_gate = nrm(ks[13], (DEPTH, D_MODEL, D_FF), D_MODEL ** -0.5)
    w_up = nrm(ks[14], (DEPTH, D_MODEL, D_FF), D_MODEL ** -0.5)
    w_down = nrm(ks[15], (DEPTH, D_FF, D_MODEL), D_FF ** -0.5)
    norm_final = 1.0 + nrm(ks[16], (D_MODEL,), 0.02)
    return {"x": x, "norm_mix": norm_mix, "w_in": w_in, "conv_w": conv_w,
            "a_log": a_log, "dt_bias": dt_bias, "o_norm_g": o_norm_g,
            "ln_v_g": ln_v_g, "ln_v_b": ln_v_b, "w_s": w_s, "b_s": b_s,
            "w_out": w_out, "norm_ffn": norm_ffn, "w_gate": w_gate, "w_up": w_up,
            "w_down": w_down, "norm_final": norm_final}


def reference(x, norm_mix, w_in, conv_w, a_log, dt_bias, o_norm_g, ln_v_g, ln_v_b,
              w_s, b_s, w_out, norm_ffn, w_gate, w_up, w_down, norm_final):
    h = x
    for l in range(DEPTH):
        hn = _rmsnorm(h, norm_mix[l])
        h = h + _hybrid_mixer(hn, w_in[l], conv_w[l], a_log[l], dt_bias[l], o_norm_g[l],
                              ln_v_g[l], ln_v_b[l], w_s[l], b_s[l], w_out[l])
        h = h + _swiglu(_rmsnorm(h, norm_ffn[l]), w_gate[l], w_up[l], w_down[l])
    return _rmsnorm(h, norm_final)
```

```python
import numpy as np
from contextlib import ExitStack
import concourse.bass as bass
import concourse.mybir as mybir
from concourse.bass_utils import run_bass_kernel_spmd

F32 = mybir.dt.float32
BF16 = mybir.dt.bfloat16
AF = mybir.ActivationFunctionType
ALU = mybir.AluOpType
AX = mybir.AxisListType

D = 1024
T = 2048
NL = 2
IN_DIM = 3080
DFF = 2816
NFC = DFF // 128
EPS = 1e-6
ST = 512
NST = T // ST


class Op:
    __slots__ = ("eng", "fn", "deps", "pos", "is_dma", "dsem", "dval", "signal",
                 "sigidx", "waits", "name", "cost", "lat", "seq", "tset")

    def __init__(self, eng, fn, name="", cost=0.1, lat=0.0):
        self.eng = eng
        self.fn = fn
        self.deps = []
        self.pos = -1
        self.is_dma = False
        self.dsem = None
        self.dval = 0
        self.signal = False
        self.sigidx = 0
        self.waits = []
        self.name = name
        self.cost = cost
        self.lat = lat
        self.seq = 0
        self.tset = None


class Prog:
    ENGS = ("pe", "act", "dve", "pool", "sp")
    XLAT = 0.25

    def __init__(self, nc, stack, reorder=True):
        self.nc = nc
        self.eng = {"pe": nc.tensor, "act": nc.scalar, "dve": nc.vector,
                    "pool": nc.gpsimd, "sp": nc.sync}
        self.stack = stack
        self.reorder = reorder
        self.sems = {e: stack.enter_context(nc.semaphore("prog_" + e)) for e in self.ENGS}
        self.dsems = {}
        self.dcnt = {}
        self.pending = []
        self.npos = {e: 0 for e in self.ENGS}
        self.nsig = {e: 0 for e in self.ENGS}
        self.last_emitted = {e: None for e in self.ENGS}
        self.last_w = {}
        self.readers = {}
        self.live_dmas = {}
        self.out_dmas = []
        self.seenpos = {e: {f: -1 for f in self.ENGS} for e in self.ENGS}
        self.seend = {e: {} for e in self.ENGS}
        self.nops = 0
        self.nwaits = 0
        self.model_time = 0.0
        self.act_set = None

    def _deps_for(self, reads, writes):
        deps = []
        for k in reads:
            w = self.last_w.get(k)
            if w is not None:
                deps.append(w)
        for k in writes:
            w = self.last_w.get(k)
            if w is not None:
                deps.append(w)
            deps.extend(self.readers.get(k, ()))
        return deps

    def _commit(self, op, reads, writes):
        for k in writes:
            self.last_w[k] = op
            self.readers[k] = []
        for k in reads:
            if k in writes:
                continue
            self.readers.setdefault(k, []).append(op)

    def _add(self, o, reads, writes, extra_deps=()):
        o.deps = self._deps_for(reads, writes) + list(extra_deps)
        o.seq = len(self.pending)
        self.pending.append(o)
        self._commit(o, reads, writes)
        return o

    def op(self, eng, fn, reads=(), writes=(), name="", cost=0.1, tset=None):
        o = Op(eng, fn, name, cost)
        o.tset = tset
        return self._add(o, reads, writes)

    def dma(self, eng, fns, semkey, reads=(), writes=(), is_output=False, name="", nbytes=0):
        issue = (0.5 if eng == "sp" else 0.65) * len(fns)
        o = Op(eng, fns, name, issue, 2.0 + nbytes / 150e3)
        o.is_dma = True
        o.dsem = semkey
        if semkey not in self.dsems:
            self.dsems[semkey] = self.stack.enter_context(
                self.nc.semaphore("dma_%d" % len(self.dsems)))
            self.dcnt[semkey] = 0
        self.dcnt[semkey] += 16 * len(fns)
        o.dval = self.dcnt[semkey]
        self._add(o, reads, writes)
        self.live_dmas[semkey] = o
        if is_output:
            self.out_dmas.append(o)
        return o

    def barrier(self):
        self.flush()
        deps = [o for e, o in self.last_emitted.items() if o is not None and e != "sp"]
        deps += list(self.live_dmas.values())
        bsp = Op("sp", lambda: self.nc.sync.nop(), "barrier_sp", 0.05)
        self._add(bsp, (), (), deps)
        for e in ("pe", "act", "dve", "pool"):
            eng = self.eng[e]
            self._add(Op(e, (lambda eng=eng: eng.nop()), "barrier_" + e, 0.05), (), (), [bsp])
        self.flush(reorder=False)
        self.last_w = {}
        self.readers = {}
        self.live_dmas = {}

    def _schedule(self, ops):
        n = len(ops)
        idx = {id(o): i for i, o in enumerate(ops)}
        preds = []
        succ = [[] for _ in range(n)]
        for i, o in enumerate(ops):
            ds = set()
            for d in o.deps:
                j = idx.get(id(d))
                if j is not None and j != i:
                    ds.add(j)
            preds.append(ds)
            for j in ds:
                succ[j].append(i)
        cp = [0.0] * n
        for i in range(n - 1, -1, -1):
            m = 0.0
            for j in succ[i]:
                if cp[j] > m:
                    m = cp[j]
            cp[i] = m + ops[i].cost + ops[i].lat
        npred = [len(p) for p in preds]
        dready = [0.0] * n
        cand = {e: set() for e in self.ENGS}
        for i in range(n):
            if npred[i] == 0:
                cand[ops[i].eng].add(i)
        free = {e: 0.0 for e in self.ENGS}
        order = []
        done = 0
        while done < n:
            best = None
            for e in self.ENGS:
                c = cand[e]
                if not c:
                    continue
                te = free[e]
                pick = None
                pick_key = None
                early = None
                early_t = None
                for i in c:
                    dr = dready[i]
                    if dr <= te:
                        if e == "act":
                            ts_ = ops[i].tset
                            k = (1 if (ts_ is None or ts_ == self.act_set) else 0, cp[i], -i)
                        else:
                            k = (1, cp[i], -i)
                        if pick is None or k > pick_key:
                            pick, pick_key = i, k
                    elif early is None or dr < early_t or (dr == early_t and i < early):
                        early, early_t = i, dr
                if pick is not None:
                    st_, ch = te, pick
                else:
                    st_, ch = early_t, early
                if best is None or st_ < best[0] or (st_ == best[0] and ch < best[1]):
                    best = (st_, ch, e)
            st_, i, e = best
            o = ops[i]
            cand[e].discard(i)
            oc = o.cost
            if e == "act" and o.tset is not None and o.tset != self.act_set:
                oc += 1.283
                self.act_set = o.tset
                self.nswitch = getattr(self, "nswitch", 0) + 1
            free[e] = st_ + oc
            fin = st_ + oc + o.lat
            order.append(i)
            done += 1
            for j in succ[i]:
                lat = 0.0 if (ops[j].eng == e and not o.is_dma) else self.XLAT
                t = fin + lat
                if t > dready[j]:
                    dready[j] = t
                npred[j] -= 1
                if npred[j] == 0:
                    cand[ops[j].eng].add(j)
        self.model_time += max(free.values())
        return [ops[i] for i in order]

    def flush(self, reorder=None):
        ops = self.pending
        self.pending = []
        if not ops:
            return
        if reorder is None:
            reorder = self.reorder
        if reorder:
            ops = self._schedule(ops)
        for o in ops:
            o.pos = self.npos[o.eng]
            self.npos[o.eng] += 1
        for o in ops:
            need = {}
            for d in o.deps:
                if d.is_dma:
                    cur = self.seend[o.eng].get(d.dsem, 0)
                    if d.dval > cur:
                        self.seend[o.eng][d.dsem] = d.dval
                        o.waits.append(("d", d.dsem, d.dval))
                    continue
                if d.eng == o.eng and o.eng in ("pe", "sp"):
                    continue
                p = need.get(d.eng)
                if p is None or d.pos > p.pos:
                    need[d.eng] = d
            for f, d in need.items():
                if d.pos > self.seenpos[o.eng][f]:
                    self.seenpos[o.eng][f] = d.pos
                    d.signal = True
                    o.waits.append(("e", f, d))
        for o in ops:
            if (not o.is_dma) and o.signal:
                self.nsig[o.eng] += 1
                o.sigidx = self.nsig[o.eng]
        for o in ops:
            eng = self.eng[o.eng]
            for w in o.waits:
                if w[0] == "d":
                    eng.wait_ge(self.dsems[w[1]], w[2])
                else:
                    eng.wait_ge(self.sems[w[1]], w[2].sigidx)
                self.nwaits += 1
            if o.is_dma:
                for fn in o.fn:
                    fn().then_inc(self.dsems[o.dsem], 16)
            else:
                inst = o.fn()
                if o.signal:
                    inst.then_inc(self.sems[o.eng], 1)
                self.last_emitted[o.eng] = o
            self.nops += 1

    def finish(self):
        self.flush()
        fin = {}
        for o in self.out_dmas:
            fin[o.dsem] = max(fin.get(o.dsem, 0), o.dval)
        for k, v in fin.items():
            self.nc.sync.wait_ge(self.dsems[k], v)


class Builder:
    def __init__(self, stop_after=None, reorder=True):
        self.stop_after = stop_after
        self.reorder = reorder
        self.nc = nc = bass.Bass("TRN2", target_bir_lowering=False)
        dr = lambda n, s, k="ExternalInput": nc.dram_tensor(n, s, F32, kind=k).ap()
        self.x = dr("x", [T, D])
        self.norm_mix = dr("norm_mix", [NL, D])
        self.w_in = dr("w_in", [NL, D, IN_DIM])
        self.conv_w = dr("conv_w", [NL, 4, 1536])
        self.a_log = dr("a_log", [NL, 4])
        self.dt_bias = dr("dt_bias", [NL, 4])
        self.o_norm_g = dr("o_norm_g", [NL, 128])
        self.ln_v_g = dr("ln_v_g", [NL, 512])
        self.ln_v_b = dr("ln_v_b", [NL, 512])
        self.w_s = dr("w_s", [NL, 4, 128, 128])
        self.b_s = dr("b_s", [NL, 4, 128])
        self.w_out = dr("w_out", [NL, D, D])
        self.norm_ffn = dr("norm_ffn", [NL, D])
        self.w_gate = dr("w_gate", [NL, D, DFF])
        self.w_up = dr("w_up", [NL, D, DFF])
        self.w_down = dr("w_down", [NL, DFF, D])
        self.norm_final = dr("norm_final", [D])
        self.cst = dr("cst", [128, 5, 128])
        self.out = dr("out", [T, D], "ExternalOutput")
        self.hA = dr("hA", [T, D], "Internal")
        self.hB = dr("hB", [T, D], "Internal")
        self.bank_pools = {'all': list(range(8)), 'a': [0, 1], 'z': [2], 'b': [3, 4, 5], 'c': [6, 7]}
        self.bank_ptr = {k: 0 for k in self.bank_pools}

    def bank(self, pool="all"):
        lst = self.bank_pools[pool]
        i = lst[self.bank_ptr[pool] % len(lst)]
        self.bank_ptr[pool] += 1
        return self.ps[i], "ps%d" % i

    @staticmethod
    def _n(ap):
        n = 1
        for d in ap.shape[1:]:
            n *= d
        return n

    def mm(self, out, lhsT, rhs, reads, writes, start=True, stop=True):
        nc = self.nc
        n = self._n(rhs)
        c = (0.03 if n <= 16 else (0.09 if n <= 128 else n / 2100.0)) * (4.0 if lhsT.dtype == F32 else 1.0)
        self.P.op("pe", lambda: nc.tensor.matmul(out, lhsT=lhsT, rhs=rhs, start=start, stop=stop),
                  reads, writes, cost=c)

    def tr(self, out, in_, ident, reads, writes):
        nc = self.nc
        self.P.op("pe", lambda: nc.tensor.transpose(out=out, in_=in_, identity=ident), reads, writes, cost=0.09)

    def act(self, out, in_, func, reads, writes, **kw):
        nc = self.nc
        c = 0.14 + self._n(out) / 1200.0 + (0.1 if "accum_out" in kw else 0.0)
        tset = {AF.Silu: "silu", AF.Gelu_apprx_tanh: "gelu", AF.Exp: "exp", AF.Tanh: "exp", AF.Ln: "ln",
                AF.Sqrt: "sqrt"}.get(func)
        self.P.op("act", lambda: nc.scalar.activation(out=out, in_=in_, func=func, **kw), reads, writes, cost=c,
                  tset=tset)

    def _vcost(self, eng, out, two_src):
        n = self._n(out)
        if eng == "dve":
            return 0.08 + n / 960.0
        return 0.1 + n / 470.0

    def tt(self, eng, out, in0, in1, op, reads, writes):
        e = self.nc.vector if eng == "dve" else self.nc.gpsimd
        self.P.op(eng, lambda: e.tensor_tensor(out=out, in0=in0, in1=in1, op=op), reads, writes,
                  cost=self._vcost(eng, out, True))

    def ts(self, eng, out, in0, s1, op0, reads, writes, s2=None, op1=None):
        e = self.nc.vector if eng == "dve" else self.nc.gpsimd
        c = self._vcost(eng, out, False)
        if op1 is None:
            self.P.op(eng, lambda: e.tensor_scalar(out=out, in0=in0, scalar1=s1, scalar2=None, op0=op0),
                      reads, writes, cost=c)
        else:
            self.P.op(eng, lambda: e.tensor_scalar(out=out, in0=in0, scalar1=s1, scalar2=s2, op0=op0, op1=op1),
                      reads, writes, cost=c)

    def cp(self, eng, out, in_, reads, writes):
        nc = self.nc
        if eng == "act":
            self.P.op("act", lambda: nc.scalar.copy(out=out, in_=in_), reads, writes,
                      cost=0.14 + self._n(out) / 1200.0)
        else:
            e = nc.vector if eng == "dve" else nc.gpsimd
            self.P.op(eng, lambda: e.tensor_copy(out=out, in_=in_), reads, writes,
                      cost=self._vcost(eng, out, False))

    def rsqrt_pool(self, out, in_, tmp, scale, eps, reads, writes, tkey):
        w = self._n(out)
        self.ts("pool", tmp, in_, scale, ALU.mult, reads, [tkey], s2=eps, op1=ALU.add)
        self.tt("pool", out, tmp, self.negh[:, 0:w], ALU.pow, [tkey, "negh"], writes)

    def recip(self, out, in_, reads, writes):
        nc = self.nc
        self.P.op("dve", lambda: nc.vector.reciprocal(out=out, in_=in_), reads, writes,
                  cost=self._vcost("dve", out, False))

    def dma1(self, eng, out, in_, semkey, reads, writes, is_output=False):
        q = self.nc.sync if eng == "sp" else self.nc.gpsimd
        nb = 128 * self._n(out) * 4
        self.P.dma(eng, [lambda: q.dma_start(out=out, in_=in_)], semkey, reads, writes, is_output, nbytes=nb)

    def build(self):
        nc = self.nc
        with ExitStack() as st:
            self.P = P = Prog(nc, st, reorder=self.reorder)
            sb = lambda n, s, d: st.enter_context(nc.sbuf_tensor(n, s, d))
            self.ps = [st.enter_context(nc.psum_tensor("ps%d" % i, [128, 512], F32)) for i in range(8)]
            self.C = C = sb("C", [128, 5, 128], F32)
            self.identb = sb("identb", [128, 128], BF16)
            self.onesb = sb("onesb", [128, 128], BF16)
            self.colv = sb("colv", [128, NL, 68], F32)
            self.negh = sb("negh", [128, 16], F32)
            P.op("pool", lambda: nc.gpsimd.memset(self.negh[:], -0.5), [], ["negh"])
            self.dma1("sp", C[:], self.cst, "cst", [], ["C"])
            self.cp("dve", self.identb[:], C[:, 0, :], ["C"], ["identb"])
            self.cp("dve", self.onesb[:], C[:, 4, :], ["C"], ["onesb"])
            self.IDF = C[:, 0, :]
            self.LOW = C[:, 1, :]
            self.LOWI = C[:, 2, :]
            self.UPI = C[:, 3, :]
            self.ONESF = C[:, 4, :]
            P.barrier()
            srcs = [self.x, self.hA, self.hB, self.hA]
            dsts = [self.hA, self.hB, self.hA, self.out]
            order = ["mix0", "ffn0", "mix1", "ffn1"]
            for ph, name in enumerate(order):
                l = ph // 2
                last = (name == self.stop_after) or ph == 3
                dst = self.out if last else dsts[ph]
                with ExitStack() as ps_:
                    if ph % 2 == 0:
                        self.mixer(l, srcs[ph], dst, ps_, is_out=last)
                    else:
                        self.ffn(l, srcs[ph], dst, ps_, final_norm=(ph == 3), is_out=last)
                    P.barrier()
                if last:
                    break
            P.finish()
        return nc

    def norm_tiles(self, src, s, HB, HN, NST_, hnT, gcol, gkey, pool="all", use_pool=True):
        nc = self.nc
        for t in range(4):
            r0 = s * ST + t * 128
            sl = t % 2
            hb = HB[sl]
            self.dma1("sp", hb[:], src[r0:r0 + 128, :], "ldh%d" % sl, [], ["hb%d" % sl])
            nst = NST_[sl]
            k = "nst%d" % sl
            self.act(HN[sl][:], hb[:], AF.Square, ["hb%d" % sl], [k + "a", "hn%d" % sl], accum_out=nst[:, 0:1])
            hn = HN[sl]
            if use_pool:
                self.rsqrt_pool(nst[:, 2:3], nst[:, 0:1], nst[:, 1:2], 1.0 / D, EPS, [k + "a"], [k + "c"], k + "b")
                self.ts("pool", hn[:], hb[:], nst[:, 2:3], ALU.mult, ["hb%d" % sl, k + "c"], ["hn%d" % sl],
                        s2=1.0, op1=ALU.mult)
            else:
                self.act(nst[:, 1:2], nst[:, 0:1], AF.Sqrt, [k + "a"], [k + "b"], scale=1.0 / D, bias=EPS)
                self.recip(nst[:, 2:3], nst[:, 1:2], [k + "b"], [k + "c"])
                self.ts("dve", hn[:], hb[:], nst[:, 2:3], ALU.mult, ["hb%d" % sl, k + "c"], ["hn%d" % sl])
            pb, pk = self.bank(pool)
            pbb = pb[:].bitcast(BF16)
            for c in range(8):
                self.tr(pbb[:, c * 128:(c + 1) * 128], hn[:, c * 128:(c + 1) * 128], self.identb[:],
                        ["hn%d" % sl, "identb"], [pk])
            self.tt("dve", hnT[:, :, t * 128:(t + 1) * 128], pbb.rearrange("p (c f) -> p c f", c=8),
                    gcol.unsqueeze(2).to_broadcast([128, 8, 128]), ALU.mult,
                    [pk, gkey], ["hnT%d" % t])

    def mixer(self, l, src, dst, st, is_out=False):
        nc = self.nc
        P = self.P
        sb = lambda n, s, d: st.enter_context(nc.sbuf_tensor("m%d_%s" % (l, n), s, d))
        C = self.C
        identb, onesb = self.identb, self.onesb
        b3 = lambda ap: ap.unsqueeze(1).to_broadcast([128, 4, 128])
        bh = lambda ap: ap.unsqueeze(2).to_broadcast([128, 4, 128])
        v4 = lambda ap: ap.rearrange("p (h f) -> p h f", h=4)

        WIN = sb("win", [128, 8, IN_DIM], BF16)
        WOUT = sb("wout", [128, 8, D], BF16)
        wv = self.w_in[l].rearrange("(c p) n -> p c n", p=128)
        WBLK = [(0, 512), (512, 1024), (1024, 1536), (1536, 2056), (2056, 3080)]
        for bi, (c0_, c1_) in enumerate(WBLK):
            fns = [lambda c=c, c0_=c0_, c1_=c1_: nc.gpsimd.dma_start(out=WIN[:, c, c0_:c1_], in_=wv[:, c, c0_:c1_])
                   for c in range(8)]
            P.dma("pool", fns, "wA%d" % bi, [], ["WIN%d" % bi], nbytes=4 * D * (c1_ - c0_))
        wo = self.w_out[l].rearrange("(c p) n -> p c n", p=128)
        P.dma("pool", [lambda c=c: nc.gpsimd.dma_start(out=WOUT[:, c, :], in_=wo[:, c, :]) for c in range(8)],
              "wB", [], ["WOUT"], nbytes=4 * D * D)

        R = sb("R", [68, 128], F32)
        P.dma("sp", [
            lambda: nc.sync.dma_start(out=R[0:8, :], in_=self.norm_mix[l].rearrange("(c p) -> c p", p=128)),
            lambda: nc.sync.dma_start(out=R[8:16, :], in_=self.norm_ffn[l].rearrange("(c p) -> c p", p=128)),
            lambda: nc.sync.dma_start(out=R[16:64, :], in_=self.conv_w[l].rearrange("j (c p) -> (j c) p", p=128)),
            lambda: nc.sync.dma_start(out=R[64:68, :], in_=self.b_s[l]),
        ], "prm", [], ["R"])
        alog = sb("alog", [128, 4], F32)
        dtb = sb("dtb", [128, 4], F32)
        gon = sb("gon", [128, 128], F32)
        lng = sb("lng", [128, 512], F32)
        lnb = sb("lnb", [128, 512], F32)
        VC = sb("VC", [128, 4, 128], F32)
        VN2 = sb("VN2", [128, 4, 128], BF16)
        wsr = VC
        P.dma("sp", [
            lambda: nc.sync.dma_start(out=alog[:], in_=self.a_log[l].partition_broadcast(128)),
            lambda: nc.sync.dma_start(out=dtb[:], in_=self.dt_bias[l].partition_broadcast(128)),
            lambda: nc.sync.dma_start(out=gon[:], in_=self.o_norm_g[l].partition_broadcast(128)),
            lambda: nc.sync.dma_start(out=lng[:], in_=self.ln_v_g[l].partition_broadcast(128)),
            lambda: nc.sync.dma_start(out=lnb[:], in_=self.ln_v_b[l].partition_broadcast(128)),
            lambda: nc.sync.dma_start(out=wsr[:], in_=self.w_s[l].rearrange("g t s -> t g s")),
        ], "prm2", [], ["alog", "dtb", "gon", "lng", "lnb", "VC"])
        colv = self.colv[:, l, :]
        pb, pk = self.bank()
        self.tr(pb[:, 0:68], R[:], C[0:68, 0, 0:68], ["R", "C"], [pk])
        self.cp("dve", colv, pb[:, 0:68], [pk], ["colv"])
        gmix = colv[:, 0:8]
        DG = sb("DG", [128, 48, 128], BF16)
        self.tt("dve", DG[:], identb[:].unsqueeze(1).to_broadcast([128, 48, 128]),
                colv[:, 16:64].unsqueeze(2).to_broadcast([128, 48, 128]), ALU.mult,
                ["identb", "colv"], ["DG"])
        wsm = VN2
        wsT = sb("wsT", [128, 4, 128], BF16)
        self.tt("dve", wsm[:], wsr[:], b3(self.LOWI), ALU.mult, ["VC", "C"], ["VN2"])
        pb, pk = self.bank()
        pbb = pb[:].bitcast(BF16)
        for g in range(4):
            self.tr(pbb[:, g * 128:(g + 1) * 128], wsm[:, g, :], identb[:], ["VN2", "identb"], [pk])
        self.cp("dve", wsT[:], v4(pbb[:, 0:512]), [pk], ["wsT"])
        nega = sb("nega", [128, 4], F32)
        self.act(nega[:], alog[:], AF.Exp, ["alog"], ["nega"])
        self.ts("dve", nega[:], nega[:], -1.0, ALU.mult, ["nega"], ["nega"])

        S32 = sb("S32", [128, 4, 128], F32)
        Sb = sb("Sb", [128, 4, 128], BF16)
        XPB = [sb("XP%d" % i, [128, ST + 3], BF16) for i in range(2)]
        HALO = sb("HALO", [128, 12, 3], BF16)
        P.op("pool", lambda: nc.gpsimd.memset(S32[:], 0.0), [], ["S32"], cost=0.7)
        P.op("pool", lambda: nc.gpsimd.memset(Sb[:], 0.0), [], ["Sb"], cost=0.7)
        P.op("pool", lambda: nc.gpsimd.memset(HALO[:], 0.0), [], ["halo%d" % c for c in range(12)], cost=0.5)

        HB = [sb("hb%d" % i, [128, D], F32) for i in range(2)]
        HN = [sb("hn%d" % i, [128, D], BF16) for i in range(2)]
        NSTt = [sb("nst%d" % i, [128, 4], F32) for i in range(2)]
        hnT = sb("hnT", [128, 8, ST], BF16)
        QKV_ = [sb("QKV%d" % i, [128, 12, ST], BF16) for i in range(2)]
        SQB = [sb("SQ%d" % i, [128, ST], BF16) for i in range(2)]
        KH_ = [sb("KH%d" % i, [128, 4, ST], BF16) for i in range(2)]
        RK = sb("RK", [128, ST], F32)
        RQ = sb("RQ", [128, 48], F32)
        D2 = lambda n, shp, dt: [sb("%s_%d" % (n, i), shp, dt) for i in range(2)]
        ZS4 = sb("ZS4", [128, 4, 512], BF16)
        UG4 = sb("UG4", [128, 4, 512], BF16)
        VG4 = sb("VG4", [128, 4, 512], BF16)
        RD_ = D2("RD", [128, 4, 128], F32)
        DEC = sb("DEC", [128, 4, 128], BF16)
        DS = sb("DS", [128, 4, 128], BF16)
        YB2 = [D2("Y%d" % i, [128, 4, 128], BF16) for i in range(2)]
        ZB2 = [D2("Z%d" % i, [128, 4, 128], BF16) for i in range(2)]
        TB2 = [D2("T%d" % i, [128, 4, 128], BF16) for i in range(2)]
        QK_ = D2("QK", [128, 4, 128], BF16)
        QKT_ = D2("QKT", [128, 4, 128], BF16)
        KBE_ = D2("KBE", [128, 4, 128], BF16)
        KDEC_ = D2("KDEC", [128, 4, 128], BF16)
        VB_ = D2("VB", [128, 4, 128], BF16)
        U = sb("U", [128, 4, 128], F32)
        WT = sb("WT", [128, 4, 128], BF16)
        VN = sb("VN", [128, 4, 128], BF16)
        O = sb("O", [128, 4, 128], F32)
        OCAT = sb("OCAT", [128, D], BF16)
        OCT = sb("OCT", [128, 8, 128], BF16)
        HR = sb("HR", [128, D], F32)
        SM_ = D2("SM", [128, 96], F32)
        SS_ = D2("SS", [128, 112], F32)
        smo = [0]
        smf = {}

        def fld(name, sl, w=4):
            if name not in smf:
                smf[name] = smo[0]
                smo[0] += w
            o_ = smf[name]
            return SM_[sl][:, o_:o_ + w], "sm%d_%s" % (sl, name)

        xpi = 0
        sqi = 0
        for s in range(NST):
            QKV, KH = QKV_[s % 2], KH_[s % 2]
            qn = lambda c, s=s: "qkv%d_%d" % (s % 2, c)
            kn = lambda h, s=s: "kh%d_%d" % (s % 2, h)
            self.norm_tiles(src, s, HB, HN, NSTt, hnT, gmix, "colv", pool="c", use_pool=(s > 0))
            hk = ["hnT%d" % t for t in range(4)]
            for cc in range(12):
                pb, pk = self.bank("a")
                for kc in range(8):
                    self.mm(pb[:], WIN[:, kc, cc * 128:(cc + 1) * 128], hnT[:, kc, :], hk + ["WIN%d" % (cc // 4)], [pk],
                            start=(kc == 0), stop=(kc == 7))
                XP = XPB[xpi % 2]
                xk = "xp%d" % (xpi % 2)
                xpi += 1
                hk_ = "halo%d" % cc
                self.cp("pool", XP[:, 0:3], HALO[:, cc, :], [hk_], [xk])
                self.cp("dve", XP[:, 3:ST + 3], pb[:], [pk], [xk])
                self.cp("pool", HALO[:, cc, :], XP[:, ST:ST + 3], [xk], [hk_])
                pb2, pk2 = self.bank("a")
                for j in range(4):
                    self.mm(pb2[:], DG[:, j * 12 + cc, :], XP[:, j:j + ST], [xk, "DG"], [pk2],
                            start=(j == 0), stop=(j == 3))
                self.act(QKV[:, cc, :], pb2[:], AF.Silu, [pk2], [qn(cc)])
            pq, pqk = self.bank("a")
            for h in range(4):
                SQ = SQB[sqi % 2]
                sk = "sq%d" % (sqi % 2)
                sqi += 1
                self.act(SQ[:], QKV[:, h, :], AF.Square, [qn(h)], [sk])
                for t in range(4):
                    self.mm(pq[:, t * 4 + h:t * 4 + h + 1], SQ[:, t * 128:(t + 1) * 128], onesb[:, 0:1],
                            [sk, "onesb"], [pqk])
            self.cp("act", RQ[:, 0:16], pq[:, 0:16], [pqk], ["rq_a"])
            self.rsqrt_pool(RQ[:, 32:48], RQ[:, 0:16], RQ[:, 16:32], 128.0, 128.0 * EPS, ["rq_a"], ["rqs"], "rq_b")
            for h in range(4):
                SQ = SQB[sqi % 2]
                sk = "sq%d" % (sqi % 2)
                sqi += 1
                self.act(SQ[:], QKV[:, 4 + h, :], AF.Square, [qn(4 + h)], [sk])
                pb, pk = self.bank("a")
                self.mm(pb[:], onesb[:], SQ[:], [sk, "onesb"], [pk])
                self.act(RK[:], pb[:], AF.Sqrt, [pk], ["RK"], bias=EPS)
                self.recip(RK[:], RK[:], ["RK"], ["RK"])
                self.tt("dve", KH[:, h, :], QKV[:, 4 + h, :], RK[:], ALU.mult, [qn(4 + h), "RK"], [kn(h)])
            SS = SS_[s % 2]
            ssk = lambda n, s=s: "ss%d_%s" % (s % 2, n)
            pg, pgk = self.bank("a")
            for t in range(4):
                for kc in range(8):
                    self.mm(pg[:, t * 8:t * 8 + 8], hnT[:, kc, t * 128:(t + 1) * 128], WIN[:, kc, 2048:2056],
                            ["hnT%d" % t, "WIN3"], [pgk], start=(kc == 0), stop=(kc == 7))
            pg3 = pg[:, 0:32].rearrange("p (t c) -> p t c", c=8)
            t4 = lambda ap: ap.rearrange("p (t c) -> p t c", c=4)
            bt4 = lambda ap: ap.unsqueeze(1).to_broadcast([128, 4, 4])
            self.act(t4(SS[:, 48:64]), pg3[:, :, 0:4], AF.Tanh, [pgk], [ssk("th")], scale=0.5)
            self.ts("dve", SS[:, 0:16], SS[:, 48:64], 0.5, ALU.mult, [ssk("th")], [ssk("bt")], s2=0.5, op1=ALU.add)
            self.ts("dve", SS[:, 16:32], SS[:, 0:16], -1.0, ALU.mult, [ssk("bt")], [ssk("nb")])
            self.tt("dve", t4(SS[:, 64:80]), pg3[:, :, 4:8], bt4(dtb[:]), ALU.add, [pgk, "dtb"], [ssk("spx")])
            self.act(SS[:, 80:96], SS[:, 64:80], AF.Exp, [ssk("spx")], [ssk("spe")])
            self.act(SS[:, 96:112], SS[:, 80:96], AF.Ln, [ssk("spe")], [ssk("spl")], bias=1.0)
            self.tt("dve", t4(SS[:, 32:48]), t4(SS[:, 96:112]), bt4(nega[:]), ALU.mult, [ssk("spl"), "nega"], [ssk("g")])
            for (c0_, c1_, wk, dstt, fn, dk) in ((1536, 2048, "WIN3", ZS4, AF.Silu, "zs%d"),
                                                (2056, 2568, "WIN4", UG4, AF.Gelu_apprx_tanh, "ug%d"),
                                                (2568, 3080, "WIN4", VG4, AF.Gelu_apprx_tanh, "vg%d")):
                for t in range(4):
                    pz, pzk = self.bank("z")
                    for kc in range(8):
                        self.mm(pz[:], hnT[:, kc, t * 128:(t + 1) * 128], WIN[:, kc, c0_:c1_], ["hnT%d" % t, wk], [pzk],
                                start=(kc == 0), stop=(kc == 7))
                    self.act(dstt[:, t, :], pz[:], fn, [pzk], [dk % t])
            khk = [kn(h) for h in range(4)]
            qk_ = [qn(h) for h in range(4)]
            vk_ = [qn(8 + h) for h in range(4)]
            for t in range(4):
                c0 = t * 128
                r0 = s * ST + c0
                hkt = "hnT%d" % t
                sl = t % 2
                K = lambda n: "%s_%d" % (n, sl)
                ZS, UG, VG, RD = ZS4[:, t, :], UG4[:, t, :], VG4[:, t, :], RD_[sl]
                QK, QKT, KBE, KDEC, VB = QK_[sl], QKT_[sl], KBE_[sl], KDEC_[sl], VB_[sl]
                YB_ = [YB2[0][sl], YB2[1][sl]]
                ZB_ = [ZB2[0][sl], ZB2[1][sl]]
                TB_ = [TB2[0][sl], TB2[1][sl]]
                BT, BTk = SS[:, 0 + t * 4:4 + t * 4], ssk("bt")
                NB, NBk = SS[:, 16 + t * 4:20 + t * 4], ssk("nb")
                G, Gk = SS[:, 32 + t * 4:36 + t * 4], ssk("g")
                pc, pck = self.bank("a")
                self.mm(pc[:, 0:4], self.UPI, G, [Gk, "C"], [pck])
                self.mm(pc[:, 4:8], self.ONESF, G, [Gk, "C"], [pck])
                GCS, GCSk = fld("gcs", sl, 8)
                self.cp("act", GCS, pc[:, 0:8], [pck], [GCSk])
                EGC, EGCk = fld("egc", sl)
                self.act(EGC, GCS[:, 0:4], AF.Exp, [GCSk], [EGCk])
                DGL, DGLk = fld("dgl", sl)
                self.tt("dve", DGL, GCS[:, 4:8], GCS[:, 0:4], ALU.subtract, [GCSk], [DGLk])
                EKD, EKDk = fld("ekd", sl)
                self.act(EKD, DGL, AF.Exp, [DGLk], [EKDk])
                EGL, EGLk = fld("egl", sl)
                self.act(EGL, GCS[:, 4:8], AF.Exp, [GCSk], [EGLk])
                BEK, BEKk = fld("bek", sl)
                self.tt("dve", BEK, BT, EGC, ALU.mult, [BTk, EGCk], [BEKk])
                RQE, RQEk = fld("rqe", sl)
                RQS = RQ[:, 32 + t * 4:36 + t * 4]
                self.tt("dve", RQE, RQS, EGC, ALU.mult, ["rqs", EGCk], [RQEk])
                self.tt("pool", RD[:], b3(self.LOW), bh(G), ALU.mult, ["C", Gk], [K("RD")])
                pd, pdk = self.bank("a")
                for h in range(4):
                    self.mm(pd[:, h * 128:(h + 1) * 128], self.UPI, RD[:, h, :], [K("RD"), "C"], [pdk])
                self.act(DEC[:], v4(pd[:]), AF.Exp, [pdk], ["DEC"])
                self.tt("pool", DS[:], DEC[:], b3(self.LOW), ALU.mult, ["DEC", "C"], ["DS"])
                self.tt("pool", DS[:], DS[:], bh(NB), ALU.mult, ["DS", NBk], ["DS"])
                self.tt("pool", DEC[:], DEC[:], b3(self.LOWI), ALU.mult, ["DEC", "C"], ["DEC"])
                self.tt("pool", DEC[:], DEC[:], bh(RQS), ALU.mult, ["DEC", "rqs"], ["DEC"])
                ykey = [K("Y0"), K("Y1")]
                zkey = [K("Z0"), K("Z1")]
                tkey = [K("T0"), K("T1")]
                Y0, Z0 = YB_[0], ZB_[0]
                pG, pGk = self.bank("a")
                for h in range(4):
                    self.mm(pG[:, h * 128:(h + 1) * 128], KH[:, h, c0:c0 + 128], KH[:, h, c0:c0 + 128], khk, [pGk])
                self.tt("dve", Z0[:], v4(pG[:]), DS[:], ALU.mult, [pGk, "DS"], [zkey[0]])
                pQ, pQk = self.bank("a")
                for h in range(4):
                    self.mm(pQ[:, h * 128:(h + 1) * 128], QKV[:, h, c0:c0 + 128], KH[:, h, c0:c0 + 128],
                            khk + qk_, [pQk])
                self.tt("dve", QK[:], v4(pQ[:]), DEC[:], ALU.mult, [pQk, "DEC"], [K("QK")])
                pb, pk = self.bank("a")
                pbb = pb[:].bitcast(BF16)
                for h in range(4):
                    self.tr(pbb[:, h * 128:(h + 1) * 128], Z0[:, h, :], identb[:], [zkey[0], "identb"], [pk])
                self.cp("act", Y0[:], v4(pbb[:, 0:512]), [pk], [ykey[0]])
                pb, pk = self.bank("a")
                pbb = pb[:].bitcast(BF16)
                for h in range(4):
                    self.tr(pbb[:, h * 128:(h + 1) * 128], QK[:, h, :], identb[:], [K("QK"), "identb"], [pk])
                self.cp("act", QKT[:], v4(pbb[:, 0:512]), [pk], [K("QKT")])
                pb, pk = self.bank("a")
                pbb = pb[:].bitcast(BF16)
                for h in range(4):
                    self.tr(pbb[:, h * 128:(h + 1) * 128], KH[:, h, c0:c0 + 128], identb[:], khk + ["identb"], [pk])
                self.tt("dve", KBE[:], v4(pbb[:, 0:512]), bh(BEK), ALU.mult, [pk, BEKk], [K("KBE")])
                self.tt("dve", KDEC[:], v4(pbb[:, 0:512]), bh(EKD), ALU.mult, [pk, EKDk], [K("KDEC")])
                pb, pk = self.bank("a")
                pbb = pb[:].bitcast(BF16)
                for h in range(4):
                    self.tr(pbb[:, h * 128:(h + 1) * 128], QKV[:, 8 + h, c0:c0 + 128], identb[:], vk_ + ["identb"], [pk])
                self.tt("dve", VB[:], v4(pbb[:, 0:512]), bh(BT), ALU.mult, [pk, BTk], [K("VB")])
                yi = zi = ti = 0
                self.tt("pool", TB_[0][:], Y0[:], b3(identb[:]), ALU.add, [ykey[0], "identb"], [tkey[0]])
                for lev in range(1, 7):
                    Yc, Zc, Tc = YB_[yi], ZB_[zi], TB_[ti]
                    Yn, Zn, Tn = YB_[1 - yi], ZB_[1 - zi], TB_[1 - ti]
                    if lev < 6:
                        pY, pYk = self.bank("b")
                        for h in range(4):
                            self.mm(pY[:, h * 128:(h + 1) * 128], Zc[:, h, :], Yc[:, h, :],
                                    [ykey[yi], zkey[zi]], [pYk])
                    pZ, pZk = self.bank("b")
                    for h in range(4):
                        self.mm(pZ[:, h * 128:(h + 1) * 128], Yc[:, h, :], Zc[:, h, :],
                                [ykey[yi], zkey[zi]], [pZk])
                    if lev < 6:
                        self.cp("act", Yn[:], v4(pY[:]), [pYk], [ykey[1 - yi]])
                    self.cp("dve", Zn[:], v4(pZ[:]), [pZk], [zkey[1 - zi]])
                    pT, pTk = self.bank("b")
                    for h in range(4):
                        self.mm(pT[:, h * 128:(h + 1) * 128], Zn[:, h, :], Tc[:, h, :],
                                [zkey[1 - zi], tkey[ti]], [pTk])
                    self.tt("dve", Tn[:], v4(pT[:]), Tc[:], ALU.add, [pTk, tkey[ti]], [tkey[1 - ti]])
                    yi, zi, ti = 1 - yi, 1 - zi, 1 - ti
                TF, TFk = TB_[ti], tkey[ti]
                pu2, pu2k = self.bank("b")
                for h in range(4):
                    self.mm(pu2[:, h * 128:(h + 1) * 128], TF[:, h, :], VB[:, h, :], [TFk, K("VB")], [pu2k])
                self.cp("act", U[:], v4(pu2[:]), [pu2k], ["U"])
                pw, pwk = self.bank("b")
                for h in range(4):
                    self.mm(pw[:, h * 128:(h + 1) * 128], KBE[:, h, :], TF[:, h, :], [TFk, K("KBE")], [pwk])
                self.cp("dve", WT[:], v4(pw[:]), [pwk], ["WT"])
                p1, p1k = self.bank("b")
                for h in range(4):
                    self.mm(p1[:, h * 128:(h + 1) * 128], WT[:, h, :], Sb[:, h, :], ["WT", "Sb"], [p1k])
                self.tt("dve", VN[:], U[:], v4(p1[:]), ALU.subtract, ["U", p1k], ["VN"])
                po1, po1k = self.bank("b")
                for h in range(4):
                    self.mm(po1[:, h * 128:(h + 1) * 128], QKV[:, h, c0:c0 + 128], Sb[:, h, :], qk_ + ["Sb"], [po1k])
                po2, po2k = self.bank("b")
                for h in range(4):
                    self.mm(po2[:, h * 128:(h + 1) * 128], QKT[:, h, :], VN[:, h, :], [K("QKT"), "VN"], [po2k])
                pS, pSk = self.bank("b")
                for h in range(4):
                    self.mm(pS[:, h * 128:(h + 1) * 128], KDEC[:, h, :], VN[:, h, :], [K("KDEC"), "VN"], [pSk])
                self.tt("pool", S32[:], S32[:], bh(EGL), ALU.mult, ["S32", EGLk], ["S32"])
                self.tt("dve", S32[:], S32[:], v4(pS[:]), ALU.add, ["S32", pSk], ["S32"])
                self.cp("act", Sb[:], S32[:], ["S32"], ["Sb"])
                self.tt("dve", O[:], v4(po1[:]), bh(RQE), ALU.mult, [po1k, RQEk], ["O"])
                self.tt("dve", O[:], O[:], v4(po2[:]), ALU.add, ["O", po2k], ["O"])
                OS, OSk = fld("os", sl)
                for h in range(4):
                    self.act(OCAT[:, 512 + h * 128:640 + h * 128], O[:, h, :], AF.Square, ["O"],
                             [OSk + str(h), "ocat_b"], accum_out=OS[:, h:h + 1])
                OR_, ORk = fld("or", sl)
                RO, ROk = fld("ro", sl)
                self.rsqrt_pool(RO, OS, OR_, 1.0 / 128, EPS, [OSk + str(h) for h in range(4)], [ROk], ORk)
                self.tt("pool", v4(ZS), v4(ZS), b3(gon[:]), ALU.mult, ["zs%d" % t, "gon"], ["zs%d" % t])
                self.tt("pool", O[:], O[:], bh(RO), ALU.mult, ["O", ROk], ["O"])
                self.tt("dve", v4(OCAT[:, 0:512]), O[:], v4(ZS), ALU.mult, ["O", "zs%d" % t], ["ocat_a"])
                S1, S1k = fld("s1", sl)
                P.op("dve", lambda S1=S1, VG=VG: nc.vector.tensor_reduce(out=S1, in_=v4(VG), axis=AX.X, op=ALU.add),
                     ["vg%d" % t], [S1k], cost=0.65)
                self.act(RD[:], v4(VG), AF.Square, ["vg%d" % t], [K("RD")])
                S2, S2k = fld("s2", sl)
                P.op("dve", lambda S2=S2, RD=RD: nc.vector.tensor_reduce(out=S2, in_=RD[:], axis=AX.X, op=ALU.add),
                     [K("RD")], [S2k], cost=0.65)
                MEAN, MEANk = fld("mean", sl)
                self.ts("dve", MEAN, S1, 1.0 / 128, ALU.mult, [S1k], [MEANk])
                M2, M2k = fld("m2", sl)
                self.tt("dve", M2, MEAN, MEAN, ALU.mult, [MEANk], [M2k])
                VAR, VARk = fld("var", sl)
                P.op("dve", lambda VAR=VAR, S2=S2, M2=M2: nc.vector.scalar_tensor_tensor(
                    out=VAR, in0=S2, scalar=1.0 / 128, in1=M2, op0=ALU.mult, op1=ALU.subtract),
                     [S2k, M2k], [VARk], cost=0.1)
                SD, SDk = fld("sd", sl)
                RS, RSk = fld("rs", sl)
                self.rsqrt_pool(RS, VAR, SD, 1.0, EPS, [VARk], [RSk], SDk)
                NM, NMk = fld("nm", sl)
                self.ts("dve", NM, MEAN, -1.0, ALU.mult, [MEANk], [NMk])
                for h in range(4):
                    self.ts("pool", VC[:, h, :], VG[:, h * 128:(h + 1) * 128], NM[:, h:h + 1], ALU.add,
                            ["vg%d" % t, NMk, RSk, "VC"], ["VC"], s2=RS[:, h:h + 1], op1=ALU.mult)
                self.tt("pool", VC[:], VC[:], v4(lng[:]), ALU.mult, ["VC", "lng"], ["VC"])
                self.tt("pool", VN2[:], VC[:], v4(lnb[:]), ALU.add, ["VC", "lnb"], ["VN2"])
                psp, pspk = self.bank("c")
                for g in range(4):
                    self.mm(psp[:, g * 128:(g + 1) * 128], wsT[:, g, :], VN2[:, g, :], ["wsT", "VN2"], [pspk])
                self.tt("dve", VC[:], v4(psp[:]), bh(colv[:, 64:68]), ALU.add, [pspk, "colv"], ["VC"])
                self.tt("dve", v4(OCAT[:, 512:1024]), VC[:], v4(UG), ALU.mult, ["VC", "ug%d" % t], ["ocat_b"])
                pb, pk = self.bank("c")
                pbb = pb[:].bitcast(BF16)
                for c in range(8):
                    self.tr(pbb[:, c * 128:(c + 1) * 128], OCAT[:, c * 128:(c + 1) * 128], identb[:],
                            ["ocat_a", "ocat_b", "identb"], [pk])
                self.cp("act", OCT[:], pbb.rearrange("p (c f) -> p c f", c=8), [pk], ["OCT"])
                self.dma1("sp", HR[:], src[r0:r0 + 128, :], "ldr", [], ["HR"])
                for n in range(2):
                    po, pok = self.bank("c")
                    for c in range(8):
                        self.mm(po[:], OCT[:, c, :], WOUT[:, c, n * 512:(n + 1) * 512], ["OCT", "WOUT"], [pok],
                                start=(c == 0), stop=(c == 7))
                    self.tt("dve", HR[:, n * 512:(n + 1) * 512], po[:], HR[:, n * 512:(n + 1) * 512], ALU.add,
                            [pok, "HR"], ["HR"])
                self.dma1("sp", dst[r0:r0 + 128, :], HR[:], "sto", ["HR"], [], is_output=is_out)

    def ffn(self, l, src, dst, st, final_norm=False, is_out=False):
        nc = self.nc
        P = self.P
        sb = lambda n, s, d: st.enter_context(nc.sbuf_tensor("f%d_%s" % (l, n), s, d))
        WG = sb("wg", [128, 8, DFF], BF16)
        WU = sb("wu", [128, 8, DFF], BF16)
        WD = sb("wd", [128, NFC, D], BF16)
        FBLK = [(0, 4), (4, 10), (10, 16), (16, 22)]
        fblk_of = {}
        for bi, (f0, f1) in enumerate(FBLK):
            for fc in range(f0, f1):
                fblk_of[fc] = bi
        for bi, (f0, f1) in enumerate(FBLK):
            for name, W, src_w in (("wG", WG, self.w_gate), ("wU", WU, self.w_up)):
                wv = src_w[l].rearrange("(c p) n -> p c n", p=128)
                fns = [lambda c=c, W=W, wv=wv, f0=f0, f1=f1: nc.gpsimd.dma_start(
                    out=W[:, c, f0 * 128:f1 * 128], in_=wv[:, c, f0 * 128:f1 * 128]) for c in range(8)]
                P.dma("pool", fns, "%s%d" % (name, bi), [], ["%s%d" % (name, bi)], nbytes=4 * D * (f1 - f0) * 128)
        wdv = self.w_down[l].rearrange("(c p) n -> p c n", p=128)
        for bi, (f0, f1) in enumerate(((0, 11), (11, 22))):
            P.dma("pool", [lambda c=c: nc.gpsimd.dma_start(out=WD[:, c, :], in_=wdv[:, c, :]) for c in range(f0, f1)],
                  "wD%d" % bi, [], ["wD%d" % bi], nbytes=4 * D * 128 * 11)
        gffn = self.colv[:, l, 8:16]
        if final_norm:
            GF = sb("gf", [128, D], F32)
            self.dma1("sp", GF[:], self.norm_final.partition_broadcast(128), "prm", [], ["GF"])
        HB = [sb("hb%d" % i, [128, D], F32) for i in range(2)]
        HN = [sb("hn%d" % i, [128, D], BF16) for i in range(2)]
        NSTt = [sb("nst%d" % i, [128, 4], F32) for i in range(2)]
        hnT = sb("hnT", [128, 8, ST], BF16)
        HFF = sb("HFF", [128, NFC, ST], BF16)
        SG = [sb("sg%d" % i, [128, ST], F32) for i in range(2)]
        HR = [sb("hr%d" % i, [128, D], F32) for i in range(2)]
        OUTT = [sb("outt%d" % i, [128, D], F32) for i in range(2)]
        FS = sb("fs", [128, 8], F32)
        for s in range(NST):
            self.norm_tiles(src, s, HB, HN, NSTt, hnT, gffn, "colvF", use_pool=False)
            hk = ["hnT%d" % t for t in range(4)]
            for fc in range(NFC):
                pg, pgk = self.bank()
                for kc in range(8):
                    self.mm(pg[:], WG[:, kc, fc * 128:(fc + 1) * 128], hnT[:, kc, :], hk + ["wG%d" % fblk_of[fc]], [pgk],
                            start=(kc == 0), stop=(kc == 7))
                pu, puk = self.bank()
                for kc in range(8):
                    self.mm(pu[:], WU[:, kc, fc * 128:(fc + 1) * 128], hnT[:, kc, :], hk + ["wU%d" % fblk_of[fc]], [puk],
                            start=(kc == 0), stop=(kc == 7))
                sg = SG[fc % 2]
                sgk = "sg%d" % (fc % 2)
                self.act(sg[:], pg[:], AF.Silu, [pgk], [sgk])
                self.tt("dve", HFF[:, fc, :], sg[:], pu[:], ALU.mult, [sgk, puk], ["hff%d" % fc])
            for t in range(4):
                r0 = s * ST + t * 128
                sl = t % 2
                self.dma1("sp", HR[sl][:], src[r0:r0 + 128, :], "ldr%d" % sl, [], ["hr%d" % sl])
                for n in range(2):
                    po, pok = self.bank()
                    for fc in range(NFC):
                        self.mm(po[:], HFF[:, fc, t * 128:(t + 1) * 128], WD[:, fc, n * 512:(n + 1) * 512],
                                ["hff%d" % fc, "wD%d" % (fc // 11)], [pok], start=(fc == 0), stop=(fc == NFC - 1))
                    self.tt("dve", OUTT[sl][:, n * 512:(n + 1) * 512], po[:], HR[sl][:, n * 512:(n + 1) * 512],
                            ALU.add, [pok, "hr%d" % sl], ["outt%d" % sl])
                if final_norm:
                    k = "fs%d" % sl
                    o = sl * 4
                    self.act(HR[sl][:], OUTT[sl][:], AF.Square, ["outt%d" % sl], [k + "a", "hr%d" % sl],
                             accum_out=FS[:, o:o + 1])
                    self.act(FS[:, o + 1:o + 2], FS[:, o:o + 1], AF.Sqrt, [k + "a"], [k + "b"], scale=1.0 / D, bias=EPS)
                    self.recip(FS[:, o + 2:o + 3], FS[:, o + 1:o + 2], [k + "b"], [k + "c"])
                    P.op("dve", lambda sl=sl, o=o: nc.vector.scalar_tensor_tensor(
                        out=OUTT[sl][:], in0=OUTT[sl][:], scalar=FS[:, o + 2:o + 3], in1=GF[:],
                        op0=ALU.mult, op1=ALU.mult), ["outt%d" % sl, k + "c", "GF"], ["outt%d" % sl])
                self.dma1("sp", dst[r0:r0 + 128, :], OUTT[sl][:], "sto%d" % sl, ["outt%d" % sl], [],
                          is_output=is_out)


def _consts():
    p = np.arange(128)[:, None]
    f = np.arange(128)[None, :]
    return np.ascontiguousarray(np.stack(
        [np.eye(128), p > f, p >= f, p <= f, np.ones((128, 128))], axis=1).astype(np.float32))


_NC_CACHE = {}


def kernel(x, norm_mix, w_in, conv_w, a_log, dt_bias, o_norm_g, ln_v_g, ln_v_b,
           w_s, b_s, w_out, norm_ffn, w_gate, w_up, w_down, norm_final, _stop_after=None):
    key = _stop_after
    if key not in _NC_CACHE:
        _NC_CACHE[key] = Builder(stop_after=_stop_after).build()
    nc = _NC_CACHE[key]
    f = lambda a: np.ascontiguousarray(np.asarray(a, dtype=np.float32))
    shared = dict(norm_mix=f(norm_mix), w_in=f(w_in), conv_w=f(conv_w), a_log=f(a_log), dt_bias=f(dt_bias),
                  o_norm_g=f(o_norm_g), ln_v_g=f(ln_v_g), ln_v_b=f(ln_v_b), w_s=f(w_s), b_s=f(b_s),
                  w_out=f(w_out), norm_ffn=f(norm_ffn), w_gate=f(w_gate), w_up=f(w_up), w_down=f(w_down),
                  norm_final=f(norm_final), cst=_consts())
    x = f(x)
    in_maps = [dict(shared, x=x[b]) for b in range(8)]
    res = run_bass_kernel_spmd(nc, in_maps, core_ids=list(range(8)))
    return np.stack([np.asarray(r["out"]) for r in res.results], axis=0).astype(np.float32)
```

```python
import numpy as np
from contextlib import ExitStack
import concourse.bass as bass
import concourse.mybir as mybir
from concourse.bass_utils import run_bass_kernel_spmd

F32 = mybir.dt.float32
BF16 = mybir.dt.bfloat16
AF = mybir.ActivationFunctionType
ALU = mybir.AluOpType
AX = mybir.AxisListType

D = 1024
T = 2048
NL = 2
IN_DIM = 3080
DFF = 2816
NFC = DFF // 128
EPS = 1e-6
ST = 512
NST = T // ST


class Op:
    __slots__ = ("eng", "fn", "deps", "pos", "is_dma", "dsem", "dval", "signal",
                 "sigidx", "waits", "name", "cost", "lat", "seq", "tset")

    def __init__(self, eng, fn, name="", cost=0.1, lat=0.0):
        self.eng = eng
        self.fn = fn
        self.deps = []
        self.pos = -1
        self.is_dma = False
        self.dsem = None
        self.dval = 0
        self.signal = False
        self.sigidx = 0
        self.waits = []
        self.name = name
        self.cost = cost
        self.lat = lat
        self.seq = 0
        self.tset = None


class Prog:
    ENGS = ("pe", "act", "dve", "pool", "sp")
    XLAT = 0.25

    def __init__(self, nc, stack, reorder=True):
        self.nc = nc
        self.eng = {"pe": nc.tensor, "act": nc.scalar, "dve": nc.vector,
                    "pool": nc.gpsimd, "sp": nc.sync}
        self.stack = stack
        self.reorder = reorder
        self.sems = {e: stack.enter_context(nc.semaphore("prog_" + e)) for e in self.ENGS}
        self.dsems = {}
        self.dcnt = {}
        self.pending = []
        self.npos = {e: 0 for e in self.ENGS}
        self.nsig = {e: 0 for e in self.ENGS}
        self.last_emitted = {e: None for e in self.ENGS}
        self.last_w = {}
        self.readers = {}
        self.live_dmas = {}
        self.out_dmas = []
        self.seenpos = {e: {f: -1 for f in self.ENGS} for e in self.ENGS}
        self.seend = {e: {} for e in self.ENGS}
        self.nops = 0
        self.nwaits = 0
        self.model_time = 0.0
        self.act_set = None

    def _deps_for(self, reads, writes):
        deps = []
        for k in reads:
            w = self.last_w.get(k)
            if w is not None:
                deps.append(w)
        for k in writes:
            w = self.last_w.get(k)
            if w is not None:
                deps.append(w)
            deps.extend(self.readers.get(k, ()))
        return deps

    def _commit(self, op, reads, writes):
        for k in writes:
            self.last_w[k] = op
            self.readers[k] = []
        for k in reads:
            if k in writes:
                continue
            self.readers.setdefault(k, []).append(op)

    def _add(self, o, reads, writes, extra_deps=()):
        o.deps = self._deps_for(reads, writes) + list(extra_deps)
        o.seq = len(self.pending)
        self.pending.append(o)
        self._commit(o, reads, writes)
        return o

    def op(self, eng, fn, reads=(), writes=(), name="", cost=0.1, tset=None):
        o = Op(eng, fn, name, cost)
        o.tset = tset
        return self._add(o, reads, writes)

    def dma(self, eng, fns, semkey, reads=(), writes=(), is_output=False, name="", nbytes=0):
        issue = (0.5 if eng == "sp" else 0.65) * len(fns)
        o = Op(eng, fns, name, issue, 2.0 + nbytes / 170e3)
        o.is_dma = True
        o.dsem = semkey
        if semkey not in self.dsems:
            self.dsems[semkey] = self.stack.enter_context(
                self.nc.semaphore("dma_%d" % len(self.dsems)))
            self.dcnt[semkey] = 0
        self.dcnt[semkey] += 16 * len(fns)
        o.dval = self.dcnt[semkey]
        self._add(o, reads, writes)
        self.live_dmas[semkey] = o
        if is_output:
            self.out_dmas.append(o)
        return o

    def barrier(self):
        self.flush()
        deps = [o for e, o in self.last_emitted.items() if o is not None and e != "sp"]
        deps += list(self.live_dmas.values())
        bsp = Op("sp", lambda: self.nc.sync.nop(), "barrier_sp", 0.05)
        self._add(bsp, (), (), deps)
        for e in ("pe", "act", "dve", "pool"):
            eng = self.eng[e]
            self._add(Op(e, (lambda eng=eng: eng.nop()), "barrier_" + e, 0.05), (), (), [bsp])
        self.flush(reorder=False)
        self.last_w = {}
        self.readers = {}
        self.live_dmas = {}

    def _schedule(self, ops):
        n = len(ops)
        idx = {id(o): i for i, o in enumerate(ops)}
        preds = []
        succ = [[] for _ in range(n)]
        for i, o in enumerate(ops):
            ds = set()
            for d in o.deps:
                j = idx.get(id(d))
                if j is not None and j != i:
                    ds.add(j)
            preds.append(ds)
            for j in ds:
                succ[j].append(i)
        cp = [0.0] * n
        for i in range(n - 1, -1, -1):
            m = 0.0
            for j in succ[i]:
                if cp[j] > m:
                    m = cp[j]
            cp[i] = m + ops[i].cost + ops[i].lat
        npred = [len(p) for p in preds]
        dready = [0.0] * n
        cand = {e: set() for e in self.ENGS}
        for i in range(n):
            if npred[i] == 0:
                cand[ops[i].eng].add(i)
        free = {e: 0.0 for e in self.ENGS}
        order = []
        done = 0
        while done < n:
            best = None
            for e in self.ENGS:
                c = cand[e]
                if not c:
                    continue
                te = free[e]
                pick = None
                pick_key = None
                early = None
                early_t = None
                for i in c:
                    dr = dready[i]
                    if dr <= te:
                        if e == "act":
                            ts_ = ops[i].tset
                            k = (1 if (ts_ is None or ts_ == self.act_set) else 0, cp[i], -i)
                        else:
                            k = (1, cp[i], -i)
                        if pick is None or k > pick_key:
                            pick, pick_key = i, k
                    elif early is None or dr < early_t or (dr == early_t and i < early):
                        early, early_t = i, dr
                if pick is not None:
                    st_, ch = te, pick
                else:
                    st_, ch = early_t, early
                if best is None or st_ < best[0] or (st_ == best[0] and ch < best[1]):
                    best = (st_, ch, e)
            st_, i, e = best
            o = ops[i]
            cand[e].discard(i)
            oc = o.cost
            if e == "act" and o.tset is not None and o.tset != self.act_set:
                oc += 1.283
                self.act_set = o.tset
            free[e] = st_ + oc
            fin = st_ + oc + o.lat
            order.append(i)
            done += 1
            for j in succ[i]:
                lat = 0.0 if (ops[j].eng == e and not o.is_dma) else self.XLAT
                t = fin + lat
                if t > dready[j]:
                    dready[j] = t
                npred[j] -= 1
                if npred[j] == 0:
                    cand[ops[j].eng].add(j)
        self.model_time += max(free.values())
        return [ops[i] for i in order]

    def flush(self, reorder=None):
        ops = self.pending
        self.pending = []
        if not ops:
            return
        if reorder is None:
            reorder = self.reorder
        if reorder:
            ops = self._schedule(ops)
        for o in ops:
            o.pos = self.npos[o.eng]
            self.npos[o.eng] += 1
        for o in ops:
            need = {}
            for d in o.deps:
                if d.is_dma:
                    cur = self.seend[o.eng].get(d.dsem, 0)
                    if d.dval > cur:
                        self.seend[o.eng][d.dsem] = d.dval
                        o.waits.append(("d", d.dsem, d.dval))
                    continue
                if d.eng == o.eng and o.eng in ("pe", "sp"):
                    continue
                p = need.get(d.eng)
                if p is None or d.pos > p.pos:
                    need[d.eng] = d
            for f, d in need.items():
                if d.pos > self.seenpos[o.eng][f]:
                    self.seenpos[o.eng][f] = d.pos
                    d.signal = True
                    o.waits.append(("e", f, d))
        for o in ops:
            if (not o.is_dma) and o.signal:
                self.nsig[o.eng] += 1
                o.sigidx = self.nsig[o.eng]
        for o in ops:
            eng = self.eng[o.eng]
            for w in o.waits:
                if w[0] == "d":
                    eng.wait_ge(self.dsems[w[1]], w[2])
                else:
                    eng.wait_ge(self.sems[w[1]], w[2].sigidx)
                self.nwaits += 1
            if o.is_dma:
                for fn in o.fn:
                    fn().then_inc(self.dsems[o.dsem], 16)
            else:
                inst = o.fn()
                if o.signal:
                    inst.then_inc(self.sems[o.eng], 1)
                self.last_emitted[o.eng] = o
            self.nops += 1

    def finish(self):
        self.flush()
        fin = {}
        for o in self.out_dmas:
            fin[o.dsem] = max(fin.get(o.dsem, 0), o.dval)
        for k, v in fin.items():
            self.nc.sync.wait_ge(self.dsems[k], v)


class Builder:
    def __init__(self, stop_after=None, reorder=True):
        self.stop_after = stop_after
        self.reorder = reorder
        self.nc = nc = bass.Bass("TRN2", target_bir_lowering=False)
        dr = lambda n, s, k="ExternalInput": nc.dram_tensor(n, s, F32, kind=k).ap()
        self.x = dr("x", [T, D])
        self.norm_mix = dr("norm_mix", [NL, D])
        self.w_in = dr("w_in", [NL, D, IN_DIM])
        self.conv_w = dr("conv_w", [NL, 4, 1536])
        self.a_log = dr("a_log", [NL, 4])
        self.dt_bias = dr("dt_bias", [NL, 4])
        self.o_norm_g = dr("o_norm_g", [NL, 128])
        self.ln_v_g = dr("ln_v_g", [NL, 512])
        self.ln_v_b = dr("ln_v_b", [NL, 512])
        self.w_s = dr("w_s", [NL, 4, 128, 128])
        self.b_s = dr("b_s", [NL, 4, 128])
        self.w_out = dr("w_out", [NL, D, D])
        self.norm_ffn = dr("norm_ffn", [NL, D])
        self.w_gate = dr("w_gate", [NL, D, DFF])
        self.w_up = dr("w_up", [NL, D, DFF])
        self.w_down = dr("w_down", [NL, DFF, D])
        self.norm_final = dr("norm_final", [D])
        self.cst = dr("cst", [128, 5, 128])
        self.out = dr("out", [T, D], "ExternalOutput")
        self.hA = dr("hA", [T, D], "Internal")
        self.hB = dr("hB", [T, D], "Internal")
        self.bank_pools = {'all': list(range(8)), 'a': [0, 1, 2], 'b': [3, 4, 5], 'c': [6, 7]}
        self.bank_ptr = {k: 0 for k in self.bank_pools}

    def bank(self, pool="all"):
        lst = self.bank_pools[pool]
        i = lst[self.bank_ptr[pool] % len(lst)]
        self.bank_ptr[pool] += 1
        return self.ps[i], "ps%d" % i

    @staticmethod
    def _n(ap):
        n = 1
        for d in ap.shape[1:]:
            n *= d
        return n

    def mm(self, out, lhsT, rhs, reads, writes, start=True, stop=True):
        nc = self.nc
        n = self._n(rhs)
        c = (0.03 if n <= 16 else (0.09 if n <= 128 else n / 2100.0)) * (4.0 if lhsT.dtype == F32 else 1.0)
        self.P.op("pe", lambda: nc.tensor.matmul(out, lhsT=lhsT, rhs=rhs, start=start, stop=stop),
                  reads, writes, cost=c)

    def tr(self, out, in_, ident, reads, writes):
        nc = self.nc
        self.P.op("pe", lambda: nc.tensor.transpose(out=out, in_=in_, identity=ident), reads, writes, cost=0.09)

    def act(self, out, in_, func, reads, writes, **kw):
        nc = self.nc
        c = 0.14 + self._n(out) / 1200.0 + (0.1 if "accum_out" in kw else 0.0)
        tset = {AF.Silu: "silu", AF.Gelu_apprx_tanh: "gelu", AF.Exp: "exp", AF.Tanh: "exp", AF.Ln: "ln",
                AF.Sqrt: "sqrt"}.get(func)
        self.P.op("act", lambda: nc.scalar.activation(out=out, in_=in_, func=func, **kw), reads, writes, cost=c,
                  tset=tset)

    def _vcost(self, eng, out, two_src):
        n = self._n(out)
        if eng == "dve":
            return 0.08 + n / 960.0
        return 0.1 + n / 470.0

    def tt(self, eng, out, in0, in1, op, reads, writes):
        e = self.nc.vector if eng == "dve" else self.nc.gpsimd
        self.P.op(eng, lambda: e.tensor_tensor(out=out, in0=in0, in1=in1, op=op), reads, writes,
                  cost=self._vcost(eng, out, True))

    def ts(self, eng, out, in0, s1, op0, reads, writes, s2=None, op1=None):
        e = self.nc.vector if eng == "dve" else self.nc.gpsimd
        c = self._vcost(eng, out, False)
        if op1 is None:
            self.P.op(eng, lambda: e.tensor_scalar(out=out, in0=in0, scalar1=s1, scalar2=None, op0=op0),
                      reads, writes, cost=c)
        else:
            self.P.op(eng, lambda: e.tensor_scalar(out=out, in0=in0, scalar1=s1, scalar2=s2, op0=op0, op1=op1),
                      reads, writes, cost=c)

    def cp(self, eng, out, in_, reads, writes):
        nc = self.nc
        if eng == "act":
            self.P.op("act", lambda: nc.scalar.copy(out=out, in_=in_), reads, writes,
                      cost=0.14 + self._n(out) / 1200.0)
        else:
            e = nc.vector if eng == "dve" else nc.gpsimd
            self.P.op(eng, lambda: e.tensor_copy(out=out, in_=in_), reads, writes,
                      cost=self._vcost(eng, out, False))

    def rsqrt_pool(self, out, in_, tmp, scale, eps, reads, writes, tkey):
        w = self._n(out)
        self.ts("pool", tmp, in_, scale, ALU.mult, reads, [tkey], s2=eps, op1=ALU.add)
        self.tt("pool", out, tmp, self.negh[:, 0:w], ALU.pow, [tkey, "negh"], writes)

    def recip(self, out, in_, reads, writes):
        nc = self.nc
        self.P.op("dve", lambda: nc.vector.reciprocal(out=out, in_=in_), reads, writes,
                  cost=self._vcost("dve", out, False))

    def dma1(self, eng, out, in_, semkey, reads, writes, is_output=False):
        q = self.nc.sync if eng == "sp" else self.nc.gpsimd
        nb = 128 * self._n(out) * 4
        self.P.dma(eng, [lambda: q.dma_start(out=out, in_=in_)], semkey, reads, writes, is_output, nbytes=nb)

    def build(self):
        nc = self.nc
        with ExitStack() as st:
            self.P = P = Prog(nc, st, reorder=self.reorder)
            sb = lambda n, s, d: st.enter_context(nc.sbuf_tensor(n, s, d))
            self.ps = [st.enter_context(nc.psum_tensor("ps%d" % i, [128, 512], F32)) for i in range(8)]
            self.C = C = sb("C", [128, 5, 128], F32)
            self.identb = sb("identb", [128, 128], BF16)
            self.onesb = sb("onesb", [128, 128], BF16)
            self.colv = sb("colv", [128, NL, 68], F32)
            self.negh = sb("negh", [128, 16], F32)
            P.op("pool", lambda: nc.gpsimd.memset(self.negh[:], -0.5), [], ["negh"])
            self.dma1("sp", C[:], self.cst, "cst", [], ["C"])
            self.cp("dve", self.identb[:], C[:, 0, :], ["C"], ["identb"])
            self.cp("dve", self.onesb[:], C[:, 4, :], ["C"], ["onesb"])
            self.IDF = C[:, 0, :]
            self.LOW = C[:, 1, :]
            self.LOWI = C[:, 2, :]
            self.UPI = C[:, 3, :]
            self.ONESF = C[:, 4, :]
            P.barrier()
            srcs = [self.x, self.hA, self.hB, self.hA]
            dsts = [self.hA, self.hB, self.hA, self.out]
            order = ["mix0", "ffn0", "mix1", "ffn1"]
            for ph, name in enumerate(order):
                l = ph // 2
                last = (name == self.stop_after) or ph == 3
                dst = self.out if last else dsts[ph]
                with ExitStack() as ps_:
                    if ph % 2 == 0:
                        self.mixer(l, srcs[ph], dst, ps_, is_out=last)
                    else:
                        self.ffn(l, srcs[ph], dst, ps_, final_norm=(ph == 3), is_out=last)
                    P.barrier()
                if last:
                    break
            P.finish()
        return nc

    def norm_tiles(self, src, s, HB, HN, NST_, hnT, gcol, gkey, pool="all", use_pool=True):
        nc = self.nc
        for t in range(4):
            r0 = s * ST + t * 128
            sl = t % 2
            hb = HB[sl]
            self.dma1("sp", hb[:], src[r0:r0 + 128, :], "ldh%d" % sl, [], ["hb%d" % sl])
            nst = NST_[sl]
            k = "nst%d" % sl
            self.act(HN[sl][:], hb[:], AF.Square, ["hb%d" % sl], [k + "a", "hn%d" % sl], accum_out=nst[:, 0:1])
            hn = HN[sl]
            if use_pool:
                self.rsqrt_pool(nst[:, 2:3], nst[:, 0:1], nst[:, 1:2], 1.0 / D, EPS, [k + "a"], [k + "c"], k + "b")
                self.ts("pool", hn[:], hb[:], nst[:, 2:3], ALU.mult, ["hb%d" % sl, k + "c"], ["hn%d" % sl],
                        s2=1.0, op1=ALU.mult)
            else:
                self.act(nst[:, 1:2], nst[:, 0:1], AF.Sqrt, [k + "a"], [k + "b"], scale=1.0 / D, bias=EPS)
                self.recip(nst[:, 2:3], nst[:, 1:2], [k + "b"], [k + "c"])
                self.ts("dve", hn[:], hb[:], nst[:, 2:3], ALU.mult, ["hb%d" % sl, k + "c"], ["hn%d" % sl])
            pb, pk = self.bank(pool)
            pbb = pb[:].bitcast(BF16)
            for c in range(8):
                self.tr(pbb[:, c * 128:(c + 1) * 128], hn[:, c * 128:(c + 1) * 128], self.identb[:],
                        ["hn%d" % sl, "identb"], [pk])
            self.tt("dve", hnT[:, :, t * 128:(t + 1) * 128], pbb.rearrange("p (c f) -> p c f", c=8),
                    gcol.unsqueeze(2).to_broadcast([128, 8, 128]), ALU.mult,
                    [pk, gkey], ["hnT%d" % t])

    def mixer(self, l, src, dst, st, is_out=False):
        nc = self.nc
        P = self.P
        sb = lambda n, s, d: st.enter_context(nc.sbuf_tensor("m%d_%s" % (l, n), s, d))
        C = self.C
        identb, onesb = self.identb, self.onesb
        b3 = lambda ap: ap.unsqueeze(1).to_broadcast([128, 4, 128])
        bh = lambda ap: ap.unsqueeze(2).to_broadcast([128, 4, 128])
        v4 = lambda ap: ap.rearrange("p (h f) -> p h f", h=4)

        WIN = sb("win", [128, 8, IN_DIM], BF16)
        WOUT = sb("wout", [128, 8, D], BF16)
        wv = self.w_in[l].rearrange("(c p) n -> p c n", p=128)
        WBLK = [(0, 512), (512, 1024), (1024, 1536), (1536, 2056), (2056, 3080)]
        for bi, (c0_, c1_) in enumerate(WBLK):
            fns = [lambda c=c, c0_=c0_, c1_=c1_: nc.gpsimd.dma_start(out=WIN[:, c, c0_:c1_], in_=wv[:, c, c0_:c1_])
                   for c in range(8)]
            P.dma("pool", fns, "wA%d" % bi, (["WIN%d" % (bi - 2)] if bi >= 2 else []), ["WIN%d" % bi],
                  nbytes=4 * D * (c1_ - c0_))
        wo = self.w_out[l].rearrange("(c p) n -> p c n", p=128)
        P.dma("pool", [lambda c=c: nc.gpsimd.dma_start(out=WOUT[:, c, :], in_=wo[:, c, :]) for c in range(8)],
              "wB", ["WIN3"], ["WOUT"], nbytes=4 * D * D)

        R = sb("R", [68, 128], F32)
        P.dma("sp", [
            lambda: nc.sync.dma_start(out=R[0:8, :], in_=self.norm_mix[l].rearrange("(c p) -> c p", p=128)),
            lambda: nc.sync.dma_start(out=R[8:16, :], in_=self.norm_ffn[l].rearrange("(c p) -> c p", p=128)),
            lambda: nc.sync.dma_start(out=R[16:64, :], in_=self.conv_w[l].rearrange("j (c p) -> (j c) p", p=128)),
            lambda: nc.sync.dma_start(out=R[64:68, :], in_=self.b_s[l]),
        ], "prm", [], ["R"])
        alog = sb("alog", [128, 4], F32)
        dtb = sb("dtb", [128, 4], F32)
        gon = sb("gon", [128, 128], F32)
        lng = sb("lng", [128, 512], F32)
        lnb = sb("lnb", [128, 512], F32)
        VC = sb("VC", [128, 4, 128], F32)
        VN2 = sb("VN2", [128, 4, 128], BF16)
        wsr = VC
        P.dma("sp", [
            lambda: nc.sync.dma_start(out=alog[:], in_=self.a_log[l].partition_broadcast(128)),
            lambda: nc.sync.dma_start(out=dtb[:], in_=self.dt_bias[l].partition_broadcast(128)),
            lambda: nc.sync.dma_start(out=gon[:], in_=self.o_norm_g[l].partition_broadcast(128)),
            lambda: nc.sync.dma_start(out=lng[:], in_=self.ln_v_g[l].partition_broadcast(128)),
            lambda: nc.sync.dma_start(out=lnb[:], in_=self.ln_v_b[l].partition_broadcast(128)),
            lambda: nc.sync.dma_start(out=wsr[:], in_=self.w_s[l].rearrange("g t s -> t g s")),
        ], "prm2", [], ["alog", "dtb", "gon", "lng", "lnb", "VC"])
        colv = self.colv[:, l, :]
        pb, pk = self.bank()
        self.tr(pb[:, 0:68], R[:], C[0:68, 0, 0:68], ["R", "C"], [pk])
        self.cp("dve", colv, pb[:, 0:68], [pk], ["colv"])
        gmix = colv[:, 0:8]
        DG = sb("DG", [128, 48, 128], BF16)
        self.tt("dve", DG[:], identb[:].unsqueeze(1).to_broadcast([128, 48, 128]),
                colv[:, 16:64].unsqueeze(2).to_broadcast([128, 48, 128]), ALU.mult,
                ["identb", "colv"], ["DG"])
        wsm = VN2
        wsT = sb("wsT", [128, 4, 128], BF16)
        self.tt("dve", wsm[:], wsr[:], b3(self.LOWI), ALU.mult, ["VC", "C"], ["VN2"])
        pb, pk = self.bank()
        pbb = pb[:].bitcast(BF16)
        for g in range(4):
            self.tr(pbb[:, g * 128:(g + 1) * 128], wsm[:, g, :], identb[:], ["VN2", "identb"], [pk])
        self.cp("dve", wsT[:], v4(pbb[:, 0:512]), [pk], ["wsT"])
        nega = sb("nega", [128, 4], F32)
        self.act(nega[:], alog[:], AF.Exp, ["alog"], ["nega"])
        self.ts("dve", nega[:], nega[:], -1.0, ALU.mult, ["nega"], ["nega"])

        S32 = sb("S32", [128, 4, 128], F32)
        Sb = sb("Sb", [128, 4, 128], BF16)
        XPB = [sb("XP%d" % i, [128, ST + 3], BF16) for i in range(2)]
        HALO = sb("HALO", [128, 12, 3], BF16)
        P.op("pool", lambda: nc.gpsimd.memset(S32[:], 0.0), [], ["S32"], cost=0.7)
        P.op("pool", lambda: nc.gpsimd.memset(Sb[:], 0.0), [], ["Sb"], cost=0.7)
        P.op("pool", lambda: nc.gpsimd.memset(HALO[:], 0.0), [], ["halo%d" % c for c in range(12)], cost=0.5)

        HB = [sb("hb%d" % i, [128, D], F32) for i in range(2)]
        HN = [sb("hn%d" % i, [128, D], BF16) for i in range(2)]
        NSTt = [sb("nst%d" % i, [128, 4], F32) for i in range(2)]
        hnT = sb("hnT", [128, 8, ST], BF16)
        QKV_ = [sb("QKV%d" % i, [128, 12, ST], BF16) for i in range(2)]
        SQB = [sb("SQ%d" % i, [128, ST], BF16) for i in range(2)]
        KH_ = [sb("KH%d" % i, [128, 4, ST], BF16) for i in range(2)]
        RK = sb("RK", [128, ST], F32)
        RQ = sb("RQ", [128, 48], F32)
        D2 = lambda n, shp, dt: [sb("%s_%d" % (n, i), shp, dt) for i in range(2)]
        ZS_ = D2("ZS", [128, 512], F32)
        UG_ = D2("UG", [128, 512], BF16)
        VG_ = D2("VG", [128, 512], F32)
        RD_ = D2("RD", [128, 4, 128], F32)
        DEC = sb("DEC", [128, 4, 128], F32)
        DS = sb("DS", [128, 4, 128], F32)
        YB2 = [D2("Y%d" % i, [128, 4, 128], BF16) for i in range(2)]
        ZB2 = [D2("Z%d" % i, [128, 4, 128], BF16) for i in range(2)]
        TB2 = [D2("T%d" % i, [128, 4, 128], BF16) for i in range(2)]
        QK_ = D2("QK", [128, 4, 128], BF16)
        QKT_ = D2("QKT", [128, 4, 128], BF16)
        KBE_ = D2("KBE", [128, 4, 128], BF16)
        KDEC_ = D2("KDEC", [128, 4, 128], BF16)
        VB_ = D2("VB", [128, 4, 128], BF16)
        U = sb("U", [128, 4, 128], F32)
        WT = sb("WT", [128, 4, 128], BF16)
        VN = sb("VN", [128, 4, 128], BF16)
        O = sb("O", [128, 4, 128], F32)
        OCAT = sb("OCAT", [128, D], BF16)
        OCT = sb("OCT", [128, 8, 128], BF16)
        HR = sb("HR", [128, D], F32)
        SM_ = D2("SM", [128, 96], F32)
        SS_ = D2("SS", [128, 112], F32)
        smo = [0]
        smf = {}

        def fld(name, sl, w=4):
            if name not in smf:
                smf[name] = smo[0]
                smo[0] += w
            o_ = smf[name]
            return SM_[sl][:, o_:o_ + w], "sm%d_%s" % (sl, name)

        xpi = 0
        sqi = 0
        for s in range(NST):
            QKV, KH = QKV_[s % 2], KH_[s % 2]
            qn = lambda c, s=s: "qkv%d_%d" % (s % 2, c)
            kn = lambda h, s=s: "kh%d_%d" % (s % 2, h)
            self.norm_tiles(src, s, HB, HN, NSTt, hnT, gmix, "colv", pool="c", use_pool=(s > 0))
            hk = ["hnT%d" % t for t in range(4)]
            for cc in range(12):
                pb, pk = self.bank("a")
                for kc in range(8):
                    self.mm(pb[:], WIN[:, kc, cc * 128:(cc + 1) * 128], hnT[:, kc, :], hk + ["WIN%d" % (cc // 4)], [pk],
                            start=(kc == 0), stop=(kc == 7))
                XP = XPB[xpi % 2]
                xk = "xp%d" % (xpi % 2)
                xpi += 1
                hk_ = "halo%d" % cc
                self.cp("pool", XP[:, 0:3], HALO[:, cc, :], [hk_], [xk])
                self.cp("dve", XP[:, 3:ST + 3], pb[:], [pk], [xk])
                self.cp("pool", HALO[:, cc, :], XP[:, ST:ST + 3], [xk], [hk_])
                pb2, pk2 = self.bank("a")
                for j in range(4):
                    self.mm(pb2[:], DG[:, j * 12 + cc, :], XP[:, j:j + ST], [xk, "DG"], [pk2],
                            start=(j == 0), stop=(j == 3))
                self.act(QKV[:, cc, :], pb2[:], AF.Silu, [pk2], [qn(cc)])
            pq, pqk = self.bank("a")
            for h in range(4):
                SQ = SQB[sqi % 2]
                sk = "sq%d" % (sqi % 2)
                sqi += 1
                self.act(SQ[:], QKV[:, h, :], AF.Square, [qn(h)], [sk])
                for t in range(4):
                    self.mm(pq[:, t * 4 + h:t * 4 + h + 1], SQ[:, t * 128:(t + 1) * 128], onesb[:, 0:1],
                            [sk, "onesb"], [pqk])
            self.cp("act", RQ[:, 0:16], pq[:, 0:16], [pqk], ["rq_a"])
            self.rsqrt_pool(RQ[:, 32:48], RQ[:, 0:16], RQ[:, 16:32], 128.0, 128.0 * EPS, ["rq_a"], ["rqs"], "rq_b")
            for h in range(4):
                SQ = SQB[sqi % 2]
                sk = "sq%d" % (sqi % 2)
                sqi += 1
                self.act(SQ[:], QKV[:, 4 + h, :], AF.Square, [qn(4 + h)], [sk])
                pb, pk = self.bank("a")
                self.mm(pb[:], onesb[:], SQ[:], [sk, "onesb"], [pk])
                self.act(RK[:], pb[:], AF.Sqrt, [pk], ["RK"], bias=EPS)
                self.recip(RK[:], RK[:], ["RK"], ["RK"])
                self.tt("dve", KH[:, h, :], QKV[:, 4 + h, :], RK[:], ALU.mult, [qn(4 + h), "RK"], [kn(h)])
            SS = SS_[s % 2]
            ssk = lambda n, s=s: "ss%d_%s" % (s % 2, n)
            pg, pgk = self.bank("a")
            for t in range(4):
                for kc in range(8):
                    self.mm(pg[:, t * 8:t * 8 + 8], hnT[:, kc, t * 128:(t + 1) * 128], WIN[:, kc, 2048:2056],
                            ["hnT%d" % t, "WIN3"], [pgk], start=(kc == 0), stop=(kc == 7))
            pg3 = pg[:, 0:32].rearrange("p (t c) -> p t c", c=8)
            t4 = lambda ap: ap.rearrange("p (t c) -> p t c", c=4)
            bt4 = lambda ap: ap.unsqueeze(1).to_broadcast([128, 4, 4])
            self.act(t4(SS[:, 48:64]), pg3[:, :, 0:4], AF.Tanh, [pgk], [ssk("th")], scale=0.5)
            self.ts("dve", SS[:, 0:16], SS[:, 48:64], 0.5, ALU.mult, [ssk("th")], [ssk("bt")], s2=0.5, op1=ALU.add)
            self.ts("dve", SS[:, 16:32], SS[:, 0:16], -1.0, ALU.mult, [ssk("bt")], [ssk("nb")])
            self.tt("dve", t4(SS[:, 64:80]), pg3[:, :, 4:8], bt4(dtb[:]), ALU.add, [pgk, "dtb"], [ssk("spx")])
            self.act(SS[:, 80:96], SS[:, 64:80], AF.Exp, [ssk("spx")], [ssk("spe")])
            self.act(SS[:, 96:112], SS[:, 80:96], AF.Ln, [ssk("spe")], [ssk("spl")], bias=1.0)
            self.tt("dve", t4(SS[:, 32:48]), t4(SS[:, 96:112]), bt4(nega[:]), ALU.mult, [ssk("spl"), "nega"], [ssk("g")])
            khk = [kn(h) for h in range(4)]
            qk_ = [qn(h) for h in range(4)]
            vk_ = [qn(8 + h) for h in range(4)]
            for t in range(4):
                c0 = t * 128
                r0 = s * ST + c0
                hkt = "hnT%d" % t
                sl = t % 2
                K = lambda n: "%s_%d" % (n, sl)
                ZS, UG, VG, RD = ZS_[sl], UG_[sl], VG_[sl], RD_[sl]
                QK, QKT, KBE, KDEC, VB = QK_[sl], QKT_[sl], KBE_[sl], KDEC_[sl], VB_[sl]
                YB_ = [YB2[0][sl], YB2[1][sl]]
                ZB_ = [ZB2[0][sl], ZB2[1][sl]]
                TB_ = [TB2[0][sl], TB2[1][sl]]
                pz, pzk = self.bank("a")
                for kc in range(8):
                    self.mm(pz[:], hnT[:, kc, c0:c0 + 128], WIN[:, kc, 1536:2048], [hkt, "WIN3"], [pzk],
                            start=(kc == 0), stop=(kc == 7))
                self.act(ZS[:], pz[:], AF.Silu, [pzk], [K("ZS")])
                BT, BTk = SS[:, 0 + t * 4:4 + t * 4], ssk("bt")
                NB, NBk = SS[:, 16 + t * 4:20 + t * 4], ssk("nb")
                G, Gk = SS[:, 32 + t * 4:36 + t * 4], ssk("g")
                pu_, puk = self.bank("a")
                for kc in range(8):
                    self.mm(pu_[:], hnT[:, kc, c0:c0 + 128], WIN[:, kc, 2056:2568], [hkt, "WIN4"], [puk],
                            start=(kc == 0), stop=(kc == 7))
                self.act(UG[:], pu_[:], AF.Gelu_apprx_tanh, [puk], [K("UG")])
                pv_, pvk = self.bank("a")
                for kc in range(8):
                    self.mm(pv_[:], hnT[:, kc, c0:c0 + 128], WIN[:, kc, 2568:3080], [hkt, "WIN4"], [pvk],
                            start=(kc == 0), stop=(kc == 7))
                self.act(VG[:], pv_[:], AF.Gelu_apprx_tanh, [pvk], [K("VG")])
                pc, pck = self.bank("a")
                self.mm(pc[:, 0:4], self.UPI, G, [Gk, "C"], [pck])
                self.mm(pc[:, 4:8], self.ONESF, G, [Gk, "C"], [pck])
                GCS, GCSk = fld("gcs", sl, 8)
                self.cp("act", GCS, pc[:, 0:8], [pck], [GCSk])
                EGC, EGCk = fld("egc", sl)
                self.act(EGC, GCS[:, 0:4], AF.Exp, [GCSk], [EGCk])
                DGL, DGLk = fld("dgl", sl)
                self.tt("dve", DGL, GCS[:, 4:8], GCS[:, 0:4], ALU.subtract, [GCSk], [DGLk])
                EKD, EKDk = fld("ekd", sl)
                self.act(EKD, DGL, AF.Exp, [DGLk], [EKDk])
                EGL, EGLk = fld("egl", sl)
                self.act(EGL, GCS[:, 4:8], AF.Exp, [GCSk], [EGLk])
                BEK, BEKk = fld("bek", sl)
                self.tt("dve", BEK, BT, EGC, ALU.mult, [BTk, EGCk], [BEKk])
                RQE, RQEk = fld("rqe", sl)
                RQS = RQ[:, 32 + t * 4:36 + t * 4]
                self.tt("dve", RQE, RQS, EGC, ALU.mult, ["rqs", EGCk], [RQEk])
                self.tt("pool", RD[:], b3(self.LOW), bh(G), ALU.mult, ["C", Gk], [K("RD")])
                pd, pdk = self.bank("a")
                for h in range(4):
                    self.mm(pd[:, h * 128:(h + 1) * 128], self.UPI, RD[:, h, :], [K("RD"), "C"], [pdk])
                self.act(DEC[:], v4(pd[:]), AF.Exp, [pdk], ["DEC"])
                self.tt("pool", DS[:], DEC[:], b3(self.LOW), ALU.mult, ["DEC", "C"], ["DS"])
                self.tt("pool", DS[:], DS[:], bh(NB), ALU.mult, ["DS", NBk], ["DS"])
                self.tt("pool", DEC[:], DEC[:], b3(self.LOWI), ALU.mult, ["DEC", "C"], ["DEC"])
                self.tt("pool", DEC[:], DEC[:], bh(RQS), ALU.mult, ["DEC", "rqs"], ["DEC"])
                ykey = [K("Y0"), K("Y1")]
                zkey = [K("Z0"), K("Z1")]
                tkey = [K("T0"), K("T1")]
                Y0, Z0 = YB_[0], ZB_[0]
                pG, pGk = self.bank("a")
                for h in range(4):
                    self.mm(pG[:, h * 128:(h + 1) * 128], KH[:, h, c0:c0 + 128], KH[:, h, c0:c0 + 128], khk, [pGk])
                self.tt("dve", Z0[:], v4(pG[:]), DS[:], ALU.mult, [pGk, "DS"], [zkey[0]])
                pQ, pQk = self.bank("a")
                for h in range(4):
                    self.mm(pQ[:, h * 128:(h + 1) * 128], QKV[:, h, c0:c0 + 128], KH[:, h, c0:c0 + 128],
                            khk + qk_, [pQk])
                self.tt("dve", QK[:], v4(pQ[:]), DEC[:], ALU.mult, [pQk, "DEC"], [K("QK")])
                pb, pk = self.bank("a")
                pbb = pb[:].bitcast(BF16)
                for h in range(4):
                    self.tr(pbb[:, h * 128:(h + 1) * 128], Z0[:, h, :], identb[:], [zkey[0], "identb"], [pk])
                self.cp("act", Y0[:], v4(pbb[:, 0:512]), [pk], [ykey[0]])
                pb, pk = self.bank("a")
                pbb = pb[:].bitcast(BF16)
                for h in range(4):
                    self.tr(pbb[:, h * 128:(h + 1) * 128], QK[:, h, :], identb[:], [K("QK"), "identb"], [pk])
                self.cp("act", QKT[:], v4(pbb[:, 0:512]), [pk], [K("QKT")])
                pb, pk = self.bank("a")
                pbb = pb[:].bitcast(BF16)
                for h in range(4):
                    self.tr(pbb[:, h * 128:(h + 1) * 128], KH[:, h, c0:c0 + 128], identb[:], khk + ["identb"], [pk])
                self.tt("dve", KBE[:], v4(pbb[:, 0:512]), bh(BEK), ALU.mult, [pk, BEKk], [K("KBE")])
                self.tt("dve", KDEC[:], v4(pbb[:, 0:512]), bh(EKD), ALU.mult, [pk, EKDk], [K("KDEC")])
                pb, pk = self.bank("a")
                pbb = pb[:].bitcast(BF16)
                for h in range(4):
                    self.tr(pbb[:, h * 128:(h + 1) * 128], QKV[:, 8 + h, c0:c0 + 128], identb[:], vk_ + ["identb"], [pk])
                self.tt("dve", VB[:], v4(pbb[:, 0:512]), bh(BT), ALU.mult, [pk, BTk], [K("VB")])
                yi = zi = ti = 0
                self.tt("pool", TB_[0][:], Y0[:], b3(identb[:]), ALU.add, [ykey[0], "identb"], [tkey[0]])
                for lev in range(1, 7):
                    Yc, Zc, Tc = YB_[yi], ZB_[zi], TB_[ti]
                    Yn, Zn, Tn = YB_[1 - yi], ZB_[1 - zi], TB_[1 - ti]
                    if lev < 6:
                        pY, pYk = self.bank("b")
                        for h in range(4):
                            self.mm(pY[:, h * 128:(h + 1) * 128], Zc[:, h, :], Yc[:, h, :],
                                    [ykey[yi], zkey[zi]], [pYk])
                    pZ, pZk = self.bank("b")
                    for h in range(4):
                        self.mm(pZ[:, h * 128:(h + 1) * 128], Yc[:, h, :], Zc[:, h, :],
                                [ykey[yi], zkey[zi]], [pZk])
                    if lev < 6:
                        self.cp("act", Yn[:], v4(pY[:]), [pYk], [ykey[1 - yi]])
                    self.cp("dve", Zn[:], v4(pZ[:]), [pZk], [zkey[1 - zi]])
                    pT, pTk = self.bank("b")
                    for h in range(4):
                        self.mm(pT[:, h * 128:(h + 1) * 128], Zn[:, h, :], Tc[:, h, :],
                                [zkey[1 - zi], tkey[ti]], [pTk])
                    self.tt("dve", Tn[:], v4(pT[:]), Tc[:], ALU.add, [pTk, tkey[ti]], [tkey[1 - ti]])
                    yi, zi, ti = 1 - yi, 1 - zi, 1 - ti
                TF, TFk = TB_[ti], tkey[ti]
                pu2, pu2k = self.bank("b")
                for h in range(4):
                    self.mm(pu2[:, h * 128:(h + 1) * 128], TF[:, h, :], VB[:, h, :], [TFk, K("VB")], [pu2k])
                self.cp("act", U[:], v4(pu2[:]), [pu2k], ["U"])
                pw, pwk = self.bank("b")
                for h in range(4):
                    self.mm(pw[:, h * 128:(h + 1) * 128], KBE[:, h, :], TF[:, h, :], [TFk, K("KBE")], [pwk])
                self.cp("dve", WT[:], v4(pw[:]), [pwk], ["WT"])
                p1, p1k = self.bank("b")
                for h in range(4):
                    self.mm(p1[:, h * 128:(h + 1) * 128], WT[:, h, :], Sb[:, h, :], ["WT", "Sb"], [p1k])
                self.tt("dve", VN[:], U[:], v4(p1[:]), ALU.subtract, ["U", p1k], ["VN"])
                po1, po1k = self.bank("b")
                for h in range(4):
                    self.mm(po1[:, h * 128:(h + 1) * 128], QKV[:, h, c0:c0 + 128], Sb[:, h, :], qk_ + ["Sb"], [po1k])
                po2, po2k = self.bank("b")
                for h in range(4):
                    self.mm(po2[:, h * 128:(h + 1) * 128], QKT[:, h, :], VN[:, h, :], [K("QKT"), "VN"], [po2k])
                pS, pSk = self.bank("b")
                for h in range(4):
                    self.mm(pS[:, h * 128:(h + 1) * 128], KDEC[:, h, :], VN[:, h, :], [K("KDEC"), "VN"], [pSk])
                self.tt("pool", S32[:], S32[:], bh(EGL), ALU.mult, ["S32", EGLk], ["S32"])
                self.tt("dve", S32[:], S32[:], v4(pS[:]), ALU.add, ["S32", pSk], ["S32"])
                self.cp("act", Sb[:], S32[:], ["S32"], ["Sb"])
                self.tt("dve", O[:], v4(po1[:]), bh(RQE), ALU.mult, [po1k, RQEk], ["O"])
                self.tt("dve", O[:], O[:], v4(po2[:]), ALU.add, ["O", po2k], ["O"])
                OS, OSk = fld("os", sl)
                for h in range(4):
                    self.act(OCAT[:, 512 + h * 128:640 + h * 128], O[:, h, :], AF.Square, ["O"],
                             [OSk + str(h), "ocat_b"], accum_out=OS[:, h:h + 1])
                OR_, ORk = fld("or", sl)
                RO, ROk = fld("ro", sl)
                self.rsqrt_pool(RO, OS, OR_, 1.0 / 128, EPS, [OSk + str(h) for h in range(4)], [ROk], ORk)
                self.tt("pool", v4(ZS[:]), v4(ZS[:]), b3(gon[:]), ALU.mult, [K("ZS"), "gon"], [K("ZS")])
                self.tt("pool", O[:], O[:], bh(RO), ALU.mult, ["O", ROk], ["O"])
                self.tt("dve", v4(OCAT[:, 0:512]), O[:], v4(ZS[:]), ALU.mult, ["O", K("ZS")], ["ocat_a"])
                S1, S1k = fld("s1", sl)
                P.op("dve", lambda S1=S1, VG=VG: nc.vector.tensor_reduce(out=S1, in_=v4(VG[:]), axis=AX.X, op=ALU.add),
                     [K("VG")], [S1k], cost=0.65)
                self.act(RD[:], v4(VG[:]), AF.Square, [K("VG")], [K("RD")])
                S2, S2k = fld("s2", sl)
                P.op("dve", lambda S2=S2, RD=RD: nc.vector.tensor_reduce(out=S2, in_=RD[:], axis=AX.X, op=ALU.add),
                     [K("RD")], [S2k], cost=0.65)
                MEAN, MEANk = fld("mean", sl)
                self.ts("dve", MEAN, S1, 1.0 / 128, ALU.mult, [S1k], [MEANk])
                M2, M2k = fld("m2", sl)
                self.tt("dve", M2, MEAN, MEAN, ALU.mult, [MEANk], [M2k])
                VAR, VARk = fld("var", sl)
                P.op("dve", lambda VAR=VAR, S2=S2, M2=M2: nc.vector.scalar_tensor_tensor(
                    out=VAR, in0=S2, scalar=1.0 / 128, in1=M2, op0=ALU.mult, op1=ALU.subtract),
                     [S2k, M2k], [VARk], cost=0.1)
                SD, SDk = fld("sd", sl)
                RS, RSk = fld("rs", sl)
                self.rsqrt_pool(RS, VAR, SD, 1.0, EPS, [VARk], [RSk], SDk)
                NM, NMk = fld("nm", sl)
                self.ts("dve", NM, MEAN, -1.0, ALU.mult, [MEANk], [NMk])
                for h in range(4):
                    self.ts("pool", VC[:, h, :], VG[:, h * 128:(h + 1) * 128], NM[:, h:h + 1], ALU.add,
                            [K("VG"), NMk, RSk, "VC"], ["VC"], s2=RS[:, h:h + 1], op1=ALU.mult)
                self.tt("pool", VC[:], VC[:], v4(lng[:]), ALU.mult, ["VC", "lng"], ["VC"])
                self.tt("pool", VN2[:], VC[:], v4(lnb[:]), ALU.add, ["VC", "lnb"], ["VN2"])
                psp, pspk = self.bank("c")
                for g in range(4):
                    self.mm(psp[:, g * 128:(g + 1) * 128], wsT[:, g, :], VN2[:, g, :], ["wsT", "VN2"], [pspk])
                self.tt("dve", VC[:], v4(psp[:]), bh(colv[:, 64:68]), ALU.add, [pspk, "colv"], ["VC"])
                self.tt("dve", v4(OCAT[:, 512:1024]), VC[:], v4(UG[:]), ALU.mult, ["VC", K("UG")], ["ocat_b"])
                pb, pk = self.bank("c")
                pbb = pb[:].bitcast(BF16)
                for c in range(8):
                    self.tr(pbb[:, c * 128:(c + 1) * 128], OCAT[:, c * 128:(c + 1) * 128], identb[:],
                            ["ocat_a", "ocat_b", "identb"], [pk])
                self.cp("act", OCT[:], pbb.rearrange("p (c f) -> p c f", c=8), [pk], ["OCT"])
                self.dma1("sp", HR[:], src[r0:r0 + 128, :], "ldr", [], ["HR"])
                for n in range(2):
                    po, pok = self.bank("c")
                    for c in range(8):
                        self.mm(po[:], OCT[:, c, :], WOUT[:, c, n * 512:(n + 1) * 512], ["OCT", "WOUT"], [pok],
                                start=(c == 0), stop=(c == 7))
                    self.tt("dve", HR[:, n * 512:(n + 1) * 512], po[:], HR[:, n * 512:(n + 1) * 512], ALU.add,
                            [pok, "HR"], ["HR"])
                self.dma1("sp", dst[r0:r0 + 128, :], HR[:], "sto", ["HR"], [], is_output=is_out)

    def ffn(self, l, src, dst, st, final_norm=False, is_out=False):
        nc = self.nc
        P = self.P
        sb = lambda n, s, d: st.enter_context(nc.sbuf_tensor("f%d_%s" % (l, n), s, d))
        WG = sb("wg", [128, 8, DFF], BF16)
        WU = sb("wu", [128, 8, DFF], BF16)
        WD = sb("wd", [128, NFC, D], BF16)
        FBLK = [(0, 3), (3, 9), (9, 15), (15, 22)]
        fblk_of = {}
        for bi, (f0, f1) in enumerate(FBLK):
            for fc in range(f0, f1):
                fblk_of[fc] = bi
        for bi, (f0, f1) in enumerate(FBLK):
            for name, W, src_w in (("wG", WG, self.w_gate), ("wU", WU, self.w_up)):
                wv = src_w[l].rearrange("(c p) n -> p c n", p=128)
                fns = [lambda c=c, W=W, wv=wv, f0=f0, f1=f1: nc.gpsimd.dma_start(
                    out=W[:, c, f0 * 128:f1 * 128], in_=wv[:, c, f0 * 128:f1 * 128]) for c in range(8)]
                P.dma("pool", fns, "%s%d" % (name, bi), (["%s%d" % (name, bi - 1)] if bi >= 1 else []),
                      ["%s%d" % (name, bi)], nbytes=4 * D * (f1 - f0) * 128)
        wdv = self.w_down[l].rearrange("(c p) n -> p c n", p=128)
        for bi, (f0, f1) in enumerate(((0, 11), (11, 22))):
            P.dma("pool", [lambda c=c: nc.gpsimd.dma_start(out=WD[:, c, :], in_=wdv[:, c, :]) for c in range(f0, f1)],
                  "wD%d" % bi, ["wG3", "wU3"] if bi == 0 else ["wD0"], ["wD%d" % bi], nbytes=4 * D * 128 * 11)
        gffn = self.colv[:, l, 8:16]
        if final_norm:
            GF = sb("gf", [128, D], F32)
            self.dma1("sp", GF[:], self.norm_final.partition_broadcast(128), "prm", [], ["GF"])
        HB = [sb("hb%d" % i, [128, D], F32) for i in range(2)]
        HN = [sb("hn%d" % i, [128, D], BF16) for i in range(2)]
        NSTt = [sb("nst%d" % i, [128, 4], F32) for i in range(2)]
        hnT = sb("hnT", [128, 8, ST], BF16)
        HFF = sb("HFF", [128, NFC, ST], BF16)
        SG = [sb("sg%d" % i, [128, ST], F32) for i in range(2)]
        HR = [sb("hr%d" % i, [128, D], F32) for i in range(2)]
        OUTT = [sb("outt%d" % i, [128, D], F32) for i in range(2)]
        FS = sb("fs", [128, 8], F32)
        for s in range(NST):
            self.norm_tiles(src, s, HB, HN, NSTt, hnT, gffn, "colvF", use_pool=False)
            hk = ["hnT%d" % t for t in range(4)]
            for fc in range(NFC):
                pg, pgk = self.bank()
                for kc in range(8):
                    self.mm(pg[:], WG[:, kc, fc * 128:(fc + 1) * 128], hnT[:, kc, :], hk + ["wG%d" % fblk_of[fc]], [pgk],
                            start=(kc == 0), stop=(kc == 7))
                pu, puk = self.bank()
                for kc in range(8):
                    self.mm(pu[:], WU[:, kc, fc * 128:(fc + 1) * 128], hnT[:, kc, :], hk + ["wU%d" % fblk_of[fc]], [puk],
                            start=(kc == 0), stop=(kc == 7))
                sg = SG[fc % 2]
                sgk = "sg%d" % (fc % 2)
                self.act(sg[:], pg[:], AF.Silu, [pgk], [sgk])
                self.tt("dve", HFF[:, fc, :], sg[:], pu[:], ALU.mult, [sgk, puk], ["hff%d" % fc])
            for t in range(4):
                r0 = s * ST + t * 128
                sl = t % 2
                self.dma1("sp", HR[sl][:], src[r0:r0 + 128, :], "ldr%d" % sl, [], ["hr%d" % sl])
                for n in range(2):
                    po, pok = self.bank()
                    for fc in range(NFC):
                        self.mm(po[:], HFF[:, fc, t * 128:(t + 1) * 128], WD[:, fc, n * 512:(n + 1) * 512],
                                ["hff%d" % fc, "wD%d" % (fc // 11)], [pok], start=(fc == 0), stop=(fc == NFC - 1))
                    self.tt("dve", OUTT[sl][:, n * 512:(n + 1) * 512], po[:], HR[sl][:, n * 512:(n + 1) * 512],
                            ALU.add, [pok, "hr%d" % sl], ["outt%d" % sl])
                if final_norm:
                    k = "fs%d" % sl
                    o = sl * 4
                    self.act(HR[sl][:], OUTT[sl][:], AF.Square, ["outt%d" % sl], [k + "a", "hr%d" % sl],
                             accum_out=FS[:, o:o + 1])
                    self.act(FS[:, o + 1:o + 2], FS[:, o:o + 1], AF.Sqrt, [k + "a"], [k + "b"], scale=1.0 / D, bias=EPS)
                    self.recip(FS[:, o + 2:o + 3], FS[:, o + 1:o + 2], [k + "b"], [k + "c"])
                    P.op("dve", lambda sl=sl, o=o: nc.vector.scalar_tensor_tensor(
                        out=OUTT[sl][:], in0=OUTT[sl][:], scalar=FS[:, o + 2:o + 3], in1=GF[:],
                        op0=ALU.mult, op1=ALU.mult), ["outt%d" % sl, k + "c", "GF"], ["outt%d" % sl])
                self.dma1("sp", dst[r0:r0 + 128, :], OUTT[sl][:], "sto%d" % sl, ["outt%d" % sl], [],
                          is_output=is_out)


def _consts():
    p = np.arange(128)[:, None]
    f = np.arange(128)[None, :]
    return np.ascontiguousarray(np.stack(
        [np.eye(128), p > f, p >= f, p <= f, np.ones((128, 128))], axis=1).astype(np.float32))


_NC_CACHE = {}


def kernel(x, norm_mix, w_in, conv_w, a_log, dt_bias, o_norm_g, ln_v_g, ln_v_b,
           w_s, b_s, w_out, norm_ffn, w_gate, w_up, w_down, norm_final, _stop_after=None):
    key = _stop_after
    if key not in _NC_CACHE:
        _NC_CACHE[key] = Builder(stop_after=_stop_after).build()
    nc = _NC_CACHE[key]
    f = lambda a: np.ascontiguousarray(np.asarray(a, dtype=np.float32))
    shared = dict(norm_mix=f(norm_mix), w_in=f(w_in), conv_w=f(conv_w), a_log=f(a_log), dt_bias=f(dt_bias),
                  o_norm_g=f(o_norm_g), ln_v_g=f(ln_v_g), ln_v_b=f(ln_v_b), w_s=f(w_s), b_s=f(b_s),
                  w_out=f(w_out), norm_ffn=f(norm_ffn), w_gate=f(w_gate), w_up=f(w_up), w_down=f(w_down),
                  norm_final=f(norm_final), cst=_consts())
    x = f(x)
    in_maps = [dict(shared, x=x[b]) for b in range(8)]
    res = run_bass_kernel_spmd(nc, in_maps, core_ids=list(range(8)))
    return np.stack([np.asarray(r["out"]) for r in res.results], axis=0).astype(np.float32)
```
